# Optimizing a Trainium2 kernel written in Bass

```python
import math
import jax, jax.numpy as jnp
from jax import lax
import numpy as np

D_MODEL = 1024
BATCH = 8
SEQ = 4096
DEPTH = 1

MLSTM_HEADS = 4
MLSTM_HEAD_DIM = D_MODEL // MLSTM_HEADS
MLSTM_WIDTH = MLSTM_HEADS * MLSTM_HEAD_DIM
MLSTM_CONV = 4
MLSTM_CHUNK = 128

ATTN_GROUPS = ((128, 1), (512, 4), (2048, 16))
ATTN_HEADS_PER_GROUP = 8
ATTN_HEAD_DIM = 64
ATTN_N_GROUPS = len(ATTN_GROUPS)
ATTN_HEADS = ATTN_N_GROUPS * ATTN_HEADS_PER_GROUP
ATTN_WIDTH = ATTN_HEADS * ATTN_HEAD_DIM
ATTN_OUT = ATTN_HEADS_PER_GROUP * ATTN_HEAD_DIM
ATTN_BLOCK = 128

N_BUCKETS = 32
MAX_DISTANCE = 2048

D_FF = -(-8 * D_MODEL // (3 * 256)) * 256

EPS = 1e-6

IN_SIZES = (MLSTM_WIDTH, MLSTM_WIDTH, MLSTM_WIDTH, 2 * MLSTM_HEADS, MLSTM_WIDTH,
            ATTN_WIDTH, ATTN_WIDTH, ATTN_WIDTH, 2 * D_MODEL)
IN_WIDTH = sum(IN_SIZES)

kernel_name = "hybrid_mlstm_dilated_swa_gated"


def _split_points(sizes):
    pts, acc = [], 0
    for s in sizes[:-1]:
        acc += s
        pts.append(acc)
    return pts


def rmsnorm(x, g):
    xf = x.astype(jnp.float32)
    r = lax.rsqrt(jnp.mean(xf * xf, axis=-1, keepdims=True) + EPS)
    return (xf * r * g.astype(jnp.float32)).astype(x.dtype)


def rel_bucket(n):
    max_exact = N_BUCKETS // 2
    nf = jnp.maximum(n, 1).astype(jnp.float32)
    large = max_exact + (jnp.log(nf / max_exact) / math.log(MAX_DISTANCE / max_exact)
                         * (N_BUCKETS - max_exact)).astype(jnp.int32)
    large = jnp.minimum(large, N_BUCKETS - 1)
    return jnp.where(n < max_exact, n, large)


def causal_depthwise_conv(x, w, b):
    c = x.shape[-1]
    y = lax.conv_general_dilated(x, w[:, None, :].astype(x.dtype), window_strides=(1,),
                                 padding=[(MLSTM_CONV - 1, 0)],
                                 dimension_numbers=('NWC', 'WIO', 'NWC'),
                                 feature_group_count=c)
    return y + b.astype(x.dtype)


def _mlstm_chunk_step(carry, inp):
    C, n, m = carry
    q, k, v, ig, lf = inp
    L = q.shape[-2]
    b = jnp.cumsum(lf, axis=-1)
    causal = jnp.tril(jnp.ones((L, L), dtype=bool))
    dmat = jnp.where(causal, b[..., :, None] - b[..., None, :] + ig[..., None, :], -jnp.inf)
    inter = b + m[..., None]
    m_t = jnp.maximum(inter, jnp.max(dmat, axis=-1))
    w_intra = jnp.exp(dmat - m_t[..., None]) * jnp.einsum('bhtd,bhsd->bhts', q, k)
    w_inter = jnp.exp(inter - m_t)
    num = (w_inter[..., None] * jnp.einsum('bhtd,bhde->bhte', q, C)
           + jnp.einsum('bhts,bhse->bhte', w_intra, v))
    den = w_inter * jnp.einsum('bhtd,bhd->bht', q, n) + jnp.sum(w_intra, axis=-1)
    h = num / jnp.maximum(jnp.abs(den), jnp.exp(-m_t))[..., None]
    b_last = b[..., -1]
    a = b_last[..., None] - b + ig
    m_new = jnp.maximum(b_last + m, jnp.max(a, axis=-1))
    decay = jnp.exp(b_last + m - m_new)
    wk = jnp.exp(a - m_new[..., None])
    C_new = decay[..., None, None] * C + jnp.einsum('bhs,bhsd,bhse->bhde', wk, k, v)
    n_new = decay[..., None] * n + jnp.einsum('bhs,bhsd->bhd', wk, k)
    return (C_new, n_new, m_new), h


def mlstm_cell(q, k, v, ig, lf):
    B, S, H, d = q.shape
    nc = S // MLSTM_CHUNK
    to_c = lambda t: t.reshape(B, nc, MLSTM_CHUNK, H, d).transpose(1, 0, 3, 2, 4)
    to_cg = lambda t: t.reshape(B, nc, MLSTM_CHUNK, H).transpose(1, 0, 3, 2)
    init = (jnp.zeros((B, H, d, d), jnp.float32), jnp.zeros((B, H, d), jnp.float32),
            jnp.zeros((B, H), jnp.float32))
    _, hs = lax.scan(_mlstm_chunk_step, init, (to_c(q), to_c(k), to_c(v), to_cg(ig), to_cg(lf)))
    return hs.transpose(1, 0, 3, 2, 4).reshape(B, S, H, d)


def dilated_window_attention(q, k, v, bias_tab, window, dilation):
    B, S, Hg, dh = q.shape
    span = window // dilation
    blk = ATTN_BLOCK
    sd = S // dilation
    nb = -(-sd // blk)
    pad = nb * blk - sd

    def to_blocks(t):
        t = t.reshape(B, sd, dilation, Hg, dh).transpose(0, 2, 3, 1, 4)
        t = jnp.pad(t, ((0, 0), (0, 0), (0, 0), (0, pad), (0, 0)))
        return t.reshape(B, dilation, Hg, nb, blk, dh)

    def with_prev(t):
        prev = jnp.pad(t, ((0, 0), (0, 0), (0, 0), (1, 0), (0, 0), (0, 0)))[:, :, :, :-1]
        return jnp.concatenate([prev, t], axis=-2)

    qb = to_blocks(q)
    kk = with_prev(to_blocks(k))
    vv = with_prev(to_blocks(v))

    q_pos = jnp.arange(blk)[:, None] + blk
    k_pos = jnp.arange(2 * blk)[None, :]
    dist = q_pos - k_pos
    key_idx = jnp.arange(nb)[:, None, None] * blk - blk + k_pos[None]
    valid = (dist >= 0) & (dist <= span) & (key_idx >= 0)
    bias = bias_tab[rel_bucket(jnp.maximum(dist, 0) * dilation)]
    bias = jnp.moveaxis(bias, -1, 0).astype(jnp.float32)[:, None]

    s = jnp.einsum('brhnqd,brhnkd->brhnqk', qb, kk) * (dh ** -0.5) + bias
    s = jnp.where(valid, s, -jnp.inf)
    mx = jnp.max(s, axis=-1, keepdims=True)
    p = jnp.exp(s - mx)
    l = jnp.sum(p, axis=-1)
    o = jnp.einsum('brhnqk,brhnkd->brhnqd', p, vv) / l[..., None]
    lse = mx[..., 0] + jnp.log(l)

    o = o.reshape(B, dilation, Hg, nb * blk, dh)[:, :, :, :sd]
    o = o.transpose(0, 3, 1, 2, 4).reshape(B, S, Hg, dh)
    lse = lse.reshape(B, dilation, Hg, nb * blk)[:, :, :, :sd]
    lse = lse.transpose(0, 3, 1, 2).reshape(B, S, Hg)
    return o, lse


def hybrid_layer(x, rel_bias, norm_mix_g, w_in, b_gate_if, conv_w, conv_b, mlstm_norm_g,
                 w_proj_a, w_proj_b, w_out, norm_ffn_g, w_gate, w_up, w_down):
    B, S, _ = x.shape
    f32 = jnp.float32
    h = rmsnorm(x, norm_mix_g)
    proj = h @ w_in
    q_m, k_m, v_m, if_pre, o_pre, q_a, k_a, v_a, gate_pre = jnp.split(
        proj, _split_points(IN_SIZES), axis=-1)

    qk = jax.nn.silu(causal_depthwise_conv(jnp.concatenate([q_m, k_m], axis=-1), conv_w, conv_b))
    q_m, k_m = jnp.split(qk.astype(f32), 2, axis=-1)
    q_m = q_m.reshape(B, S, MLSTM_HEADS, MLSTM_HEAD_DIM) * (MLSTM_HEAD_DIM ** -0.5)
    k_m = k_m.reshape(B, S, MLSTM_HEADS, MLSTM_HEAD_DIM)
    v_m = v_m.astype(f32).reshape(B, S, MLSTM_HEADS, MLSTM_HEAD_DIM)
    if_pre = if_pre.astype(f32) + b_gate_if.astype(f32)
    ig = if_pre[..., :MLSTM_HEADS]
    lf = jax.nn.log_sigmoid(if_pre[..., MLSTM_HEADS:])
    hm = mlstm_cell(q_m, k_m, v_m, ig, lf)
    mu = jnp.mean(hm, axis=-1, keepdims=True)
    var = jnp.mean(jnp.square(hm - mu), axis=-1, keepdims=True)
    hm = (hm - mu) * lax.rsqrt(var + EPS)
    hm = hm.reshape(B, S, MLSTM_WIDTH) * mlstm_norm_g.astype(f32)
    y_a = (jax.nn.sigmoid(o_pre.astype(f32)) * hm).astype(x.dtype)

    shp = (B, S, ATTN_N_GROUPS, ATTN_HEADS_PER_GROUP, ATTN_HEAD_DIM)
    q_a = q_a.astype(f32).reshape(shp)
    k_a = k_a.astype(f32).reshape(shp)
    v_a = v_a.astype(f32).reshape(shp)
    outs, lses = [], []
    for g, (window, dilation) in enumerate(ATTN_GROUPS):
        tab = rel_bias[:, g * ATTN_HEADS_PER_GROUP:(g + 1) * ATTN_HEADS_PER_GROUP]
        o_g, lse_g = dilated_window_attention(q_a[:, :, g], k_a[:, :, g], v_a[:, :, g],
                                              tab, window, dilation)
        outs.append(o_g)
        lses.append(lse_g)
    wts = jax.nn.softmax(jnp.stack(lses, axis=0), axis=0)
    y_b = jnp.sum(wts[..., None] * jnp.stack(outs, axis=0), axis=0)
    y_b = y_b.reshape(B, S, ATTN_OUT).astype(x.dtype)

    g_a, g_b = jnp.split(jax.nn.sigmoid(gate_pre), 2, axis=-1)
    merged = g_a * (y_a @ w_proj_a) + g_b * (y_b @ w_proj_b)
    x = x + merged @ w_out

    hf = rmsnorm(x, norm_ffn_g)
    x = x + (jax.nn.silu(hf @ w_gate) * (hf @ w_up)) @ w_down
    return x


def setup_inputs(seed: int = 0) -> dict:
    key = jax.random.key(seed)
    ks = jax.random.split(key, 20)
    nrm = lambda k, shape, scale: jax.random.normal(k, shape, jnp.float32) * scale
    x = jax.random.normal(ks[0], (BATCH, SEQ, D_MODEL), jnp.float32)
    norm_mix_g = 1.0 + nrm(ks[1], (DEPTH, D_MODEL), 0.02)
    w_in = nrm(ks[2], (DEPTH, D_MODEL, IN_WIDTH), D_MODEL ** -0.5)
    ig_bias = nrm(ks[3], (DEPTH, MLSTM_HEADS), 0.1)
    fg_bias = jnp.linspace(3.0, 6.0, MLSTM_HEADS, dtype=jnp.float32)[None] + nrm(ks[4], (DEPTH, MLSTM_HEADS), 0.1)
    b_gate_if = jnp.concatenate([ig_bias, fg_bias], axis=-1)
    conv_w = nrm(ks[5], (DEPTH, MLSTM_CONV, 2 * MLSTM_WIDTH), MLSTM_CONV ** -0.5)
    conv_b = nrm(ks[6], (DEPTH, 2 * MLSTM_WIDTH), 0.02)
    mlstm_norm_g = 1.0 + nrm(ks[7], (DEPTH, MLSTM_WIDTH), 0.02)
    w_proj_a = nrm(ks[8], (DEPTH, MLSTM_WIDTH, D_MODEL), MLSTM_WIDTH ** -0.5)
    w_proj_b = nrm(ks[9], (DEPTH, ATTN_OUT, D_MODEL), ATTN_OUT ** -0.5)
    w_out = nrm(ks[10], (DEPTH, D_MODEL, D_MODEL), D_MODEL ** -0.5)
    norm_ffn_g = 1.0 + nrm(ks[11], (DEPTH, D_MODEL), 0.02)
    w_gate = nrm(ks[12], (DEPTH, D_MODEL, D_FF), D_MODEL ** -0.5)
    w_up = nrm(ks[13], (DEPTH, D_MODEL, D_FF), D_MODEL ** -0.5)
    w_down = nrm(ks[14], (DEPTH, D_FF, D_MODEL), D_FF ** -0.5)
    rel_bias = nrm(ks[15], (N_BUCKETS, ATTN_HEADS), 0.5)
    norm_final_g = 1.0 + nrm(ks[16], (D_MODEL,), 0.02)
    return {"x": x, "norm_mix_g": norm_mix_g, "w_in": w_in, "b_gate_if": b_gate_if,
            "conv_w": conv_w, "conv_b": conv_b, "mlstm_norm_g": mlstm_norm_g,
            "w_proj_a": w_proj_a, "w_proj_b": w_proj_b, "w_out": w_out,
            "norm_ffn_g": norm_ffn_g, "w_gate": w_gate, "w_up": w_up, "w_down": w_down,
            "rel_bias": rel_bias, "norm_final_g": norm_final_g}


def reference(x, norm_mix_g, w_in, b_gate_if, conv_w, conv_b, mlstm_norm_g, w_proj_a,
              w_proj_b, w_out, norm_ffn_g, w_gate, w_up, w_down, rel_bias, norm_final_g):
    for layer in range(DEPTH):
        x = hybrid_layer(x, rel_bias, norm_mix_g[layer], w_in[layer], b_gate_if[layer],
                         conv_w[layer], conv_b[layer], mlstm_norm_g[layer], w_proj_a[layer],
                         w_proj_b[layer], w_out[layer], norm_ffn_g[layer], w_gate[layer],
                         w_up[layer], w_down[layer])
    return rmsnorm(x, norm_final_g)
```

```python
import contextlib
import numpy as np
import ml_dtypes
import concourse.bass as bass
import concourse.mybir as mybir
from concourse.bass_utils import run_bass_kernel_spmd

F32 = mybir.dt.float32
BF16 = mybir.dt.bfloat16
AF = mybir.ActivationFunctionType
ALU = mybir.AluOpType
AX = mybir.AxisListType

S = 4096
D = 1024
NT = S // 128
EPS = 1e-6
N_DMA_SEMS = 24


class Buf:
    __slots__ = ("name", "w", "r")

    def __init__(self, name=""):
        self.name = name
        self.w = None
        self.r = {}


class KB:
    def __init__(self, nc, es):
        self.nc = nc
        self.es = es
        self.eng = {"pe": nc.tensor, "act": nc.scalar, "dve": nc.vector,
                    "pool": nc.gpsimd, "sp": nc.sync}
        self.sem = {}
        self.cnt = {}
        for e in self.eng:
            self.sem[e] = es.enter_context(nc.semaphore("s_" + e))
            self.cnt[e] = 0
        for i in range(N_DMA_SEMS):
            k = ("dma", i)
            self.sem[k] = es.enter_context(nc.semaphore("s_dma%d" % i))
            self.cnt[k] = 0
        self.waited = {e: {} for e in self.eng}
        self.rr = 0
        self.n_ins = 0
        self.deferred = []

    def sbuf(self, name, shape, dt):
        return self.es.enter_context(self.nc.sbuf_tensor(name, list(shape), dt))

    def sbuf_in(self, es, name, shape, dt):
        return es.enter_context(self.nc.sbuf_tensor(name, list(shape), dt))

    def psum(self, name, shape, dt):
        return self.es.enter_context(self.nc.psum_tensor(name, list(shape), dt))

    def _deps(self, e, reads, writes):
        deps = {}

        def add(k, v):
            if deps.get(k, 0) < v:
                deps[k] = v
        for b in reads:
            if b.w is not None:
                add(*b.w)
        for b in writes:
            if b.w is not None and b.w[0] != e:
                add(*b.w)
            for k, v in b.r.items():
                if k != e:
                    add(k, v)
        if e == "pe":
            deps.pop("pe", None)
        return deps

    def _wait(self, e, deps):
        eng = self.eng[e]
        w = self.waited[e]
        for k, v in deps.items():
            if w.get(k, 0) >= v:
                continue
            eng.wait_ge(self.sem[k], v)
            w[k] = v

    def op(self, e, fn, reads=(), writes=()):
        self._wait(e, self._deps(e, reads, writes))
        ins = fn(self.eng[e])
        self.cnt[e] += 1
        c = self.cnt[e]
        ins.then_inc(self.sem[e], 1)
        for b in reads:
            b.r[e] = c
        for b in writes:
            b.w = (e, c)
            b.r = {}
        self.n_ins += 1
        return ins

    def dma(self, out, in_, reads=(), writes=(), q="sp"):
        k = ("dma", self.rr)
        self.rr = (self.rr + 1) % N_DMA_SEMS
        deps = self._deps(q, reads, writes)
        if self.cnt[k] > 0 and deps.get(k, 0) < self.cnt[k]:
            deps[k] = self.cnt[k]
        self._wait(q, deps)
        ins = self.eng[q].dma_start(out=out, in_=in_)
        self.cnt[k] += 16
        c = self.cnt[k]
        ins.then_inc(self.sem[k], 16)
        for b in reads:
            b.r[k] = c
        for b in writes:
            b.w = (k, c)
            b.r = {}
        self.n_ins += 1
        return ins

    def dma_later(self, lag, out, in_, reads=(), q="sp"):
        self.deferred.append([lag, out, in_, list(reads), q])

    def tick(self):
        keep = []
        for d in self.deferred:
            d[0] -= 1
            if d[0] <= 0:
                self.dma(d[1], d[2], reads=d[3], q=d[4])
            else:
                keep.append(d)
        self.deferred = keep

    def flush(self):
        for d in self.deferred:
            self.dma(d[1], d[2], reads=d[3], q=d[4])
        self.deferred = []

    def finish(self):
        for i in range(N_DMA_SEMS):
            k = ("dma", i)
            if self.cnt[k] > 0:
                self.eng["sp"].wait_ge(self.sem[k], self.cnt[k])
        for e in ("pe", "act", "dve", "pool"):
            if self.cnt[e] > 0:
                self.eng["sp"].wait_ge(self.sem[e], self.cnt[e])

    def barrier(self):
        self.flush()
        keys = list(self.sem.keys())
        for e in self.eng:
            w = self.waited[e]
            for k in keys:
                if k == e:
                    continue
                if self.cnt[k] > w.get(k, 0):
                    self.eng[e].wait_ge(self.sem[k], self.cnt[k])
                    w[k] = self.cnt[k]


OFF_QM, OFF_KM, OFF_VM, OFF_IF, OFF_OM, OFF_QA, OFF_KA, OFF_VA, OFF_GATE = (
    0, 1024, 2048, 3072, 3080, 4104, 5640, 7176, 8712)
IN_W = 10760
DFF = 2816
NFF = DFF // 128
LN16 = float(np.log(16.0))
DILS = (1, 4, 16)


def _tok(d, r, m, n=1):
    st = r + d * 128 * m
    return slice(st, st + d * (128 * n - 1) + 1, d)


class Prog:
    def __init__(self, dbg=False, stop_after=99):
        self.dbg = dbg
        self.stop_after = stop_after
        nc = bass.Bass("TRN2", target_bir_lowering=False)
        self.nc = nc
        self.inp = {}
        self.out = {}

        def din(name, shape, dt=F32):
            self.inp[name] = nc.dram_tensor(name, list(shape), dt, kind="ExternalInput").ap()
            return self.inp[name]

        def dscr(name, shape, dt):
            kind = "ExternalOutput" if dbg else "Internal"
            t = nc.dram_tensor(name, list(shape), dt, kind=kind).ap()
            if dbg:
                self.out[name] = t
            return t

        self.x = din("x", [S, D])
        self.w_in = din("w_in", [D, IN_W])
        self.g_mix = din("g_mix", [128, 8])
        self.b_if = din("b_if", [128, 8])
        self.conv_w = din("conv_w", [128, 16, 4])
        self.conv_b = din("conv_b", [128, 16])
        self.g_ml = din("g_ml", [128, 8])
        self.w_a = din("w_a", [D, D])
        self.w_b = din("w_b", [512, D])
        self.w_o = din("w_o", [D, D])
        self.g_ffn = din("g_ffn", [128, 8])
        self.w_gate = din("w_gate", [D, DFF])
        self.w_up = din("w_up", [D, DFF])
        self.w_down = din("w_down", [DFF, D])
        self.g_fin = din("g_fin", [128, 1024])
        self.biasT = din("biasT", [128, 3 * 8 * 256])
        self.c_ident = din("c_ident", [128, 128], BF16)
        self.c_identf = din("c_identf", [128, 128])
        self.c_tri = din("c_tri", [128, 128])
        self.c_negm = din("c_negm", [128, 128])
        self.c_ones = din("c_ones", [128, 128])

        self.y = nc.dram_tensor("y", [S, D], F32, kind="ExternalOutput").ap()
        self.ybT_d = dscr("ybT_d", [512, S], BF16)
        self.qT_d = dscr("qT_d", [1024, S], BF16)
        self.kT_d = dscr("kT_d", [1024, S], BF16)
        self.v_d = dscr("v_d", [S, 1024], BF16)
        self.og_d = dscr("og_d", [S, 1024], BF16)
        self.yaT_d = dscr("yaT_d", [1024, S], BF16)
        self.mT_d = dscr("mT_d", [1024, S], BF16)
        self.x1_d = dscr("x1_d", [S, D], F32)
        self.hfT_d = dscr("hfT_d", [1024, S], BF16)

        with contextlib.ExitStack() as es:
            kb = KB(nc, es)
            self.kb = kb
            self.ps = [kb.psum("ps%d" % i, [128, 512], F32) for i in range(8)]
            self.pb = [Buf("ps%d" % i) for i in range(8)]
            self.psi = 0
            self.consts(es)
            with contextlib.ExitStack() as es_mix:
                self.hT = kb.sbuf_in(es_mix, "hT", [128, 8, S], BF16)
                self.b_hT = [Buf("hT%d" % i) for i in range(8)]
                self.scoped(self.phase1, "phase1")
                if stop_after >= 2:
                    self.scoped(self.phase2, "phase2")
                if stop_after >= 3:
                    self.scoped(self.phase3a, "phase3a")
                if stop_after >= 4:
                    self.scoped(self.phase3b, "phase3b")
                if stop_after >= 5:
                    self.scoped(self.phase4, "phase4")
                if getattr(self, "es3", None) is not None:
                    self.es3.close()
                kb.barrier()
            if stop_after >= 5:
                self.scoped(self.phase4b, "phase4b")
            if stop_after >= 6:
                self.scoped(self.phase5, "phase5")
            kb.barrier()
            kb.finish()

    def scoped(self, fn, name):
        with self.nc.named_scope(name):
            fn()

    def nps(self):
        lo = getattr(self, "ps_lo", 0)
        i = self.psi
        if i < lo:
            i = lo
        self.psi = i + 1 if i + 1 < 8 else lo
        return self.ps[i], self.pb[i]

    def consts(self, es):
        kb = self.kb

        def ld(name, src, shape, dt=F32):
            t = kb.sbuf(name, shape, dt)
            b = Buf(name)
            kb.dma(t[:], src, writes=[b])
            return t, b
        self.ident, self.b_ident = ld("ident", self.c_ident[:, :], [128, 128], BF16)
        self.identf, self.b_identf = ld("identf", self.c_identf[:, :], [128, 128])
        self.tri, self.b_tri = ld("tri", self.c_tri[:, :], [128, 128])
        self.negm, self.b_negm = ld("negm", self.c_negm[:, :], [128, 128])
        self.ones, self.b_ones = ld("ones", self.c_ones[:, :], [128, 128])
        self.gmix, self.b_gmix = ld("gmix", self.g_mix[:, :], [128, 8])
        self.gffn, self.b_gffn = ld("gffn", self.g_ffn[:, :], [128, 8])
        self.gml, self.b_gml = ld("gml", self.g_ml[:, :], [128, 8])

    def make_wl(self, es, nbuf, kc, n, name):
        kb = self.kb
        self.wl_st = [kb.sbuf_in(es, "%s_st%d" % (name, i), [128, kc, n], F32) for i in range(nbuf)]
        self.wl_b = [Buf() for _ in range(nbuf)]
        self.wl_i = 0

    def load_w_dma(self, src, kc, n):
        i = self.wl_i
        self.wl_i = (i + 1) % len(self.wl_st)
        st, sb = self.wl_st[i], self.wl_b[i]
        self.kb.dma(st[:, 0:kc, 0:n], src, writes=[sb])
        return (st, sb, kc, n)

    def load_w_cast(self, tok, dst, dst_buf, gsc=None, gsc_buf=None, eng=None):
        kb = self.kb
        st, sb, kc, n = tok
        if eng is None:
            self.wl_e = 1 - getattr(self, "wl_e", 0)
            eng = "act" if self.wl_e else "dve"
        if gsc is None:
            kb.op(eng, lambda e: (e.copy if eng == "act" else e.tensor_copy)(out=dst, in_=st[:, 0:kc, 0:n]),
                  reads=[sb], writes=[dst_buf])
        else:
            for c in range(kc):
                if eng == "act":
                    kb.op(eng, lambda e: e.activation(out=dst[:, c, :], in_=st[:, c, 0:n], func=AF.Copy,
                                                      scale=gsc[:, c:c + 1]),
                          reads=[sb, gsc_buf], writes=[dst_buf])
                else:
                    kb.op(eng, lambda e: e.tensor_scalar(out=dst[:, c, :], in0=st[:, c, 0:n], scalar1=gsc[:, c:c + 1],
                                                         scalar2=None, op0=ALU.mult),
                          reads=[sb, gsc_buf], writes=[dst_buf])

    def load_w(self, src, dst, dst_buf, kc, n, gsc=None, gsc_buf=None, eng=None):
        self.load_w_cast(self.load_w_dma(src, kc, n), dst, dst_buf, gsc, gsc_buf, eng)

    def win(self, c0, n):
        return self.w_in[:, c0:c0 + n].rearrange("(c p) n -> p c n", p=128)

    def rms(self, xt, b_xt, junk, b_junk, ss, b_ss):
        kb = self.kb
        kb.op("act", lambda e: e.activation(out=junk[:], in_=xt, func=AF.Square, scale=float(D ** -0.5),
                                            accum_out=ss[:]), reads=[b_xt], writes=[b_junk, b_ss])
        kb.op("act", lambda e: e.activation(out=ss[:], in_=ss[:], func=AF.Sqrt, bias=EPS), reads=[b_ss], writes=[b_ss])
        kb.op("dve", lambda e: e.reciprocal(out=ss[:], in_=ss[:]), reads=[b_ss], writes=[b_ss])

    def phase1(self):
        kb = self.kb
        with contextlib.ExitStack() as es:
            xt = [kb.sbuf_in(es, "p1x%d" % i, [128, D], F32) for i in range(4)]
            bx = [Buf() for _ in range(4)]
            junk = kb.sbuf_in(es, "p1j", [128, D], BF16)
            bj = Buf()
            ss = [kb.sbuf_in(es, "p1s%d" % i, [128, 1], F32) for i in range(2)]
            bs = [Buf() for _ in range(2)]
            hn = [kb.sbuf_in(es, "p1h%d" % i, [128, D], BF16) for i in range(2)]
            bh = [Buf() for _ in range(2)]
            def tail(t):
                i = t % 2
                ps, pb = self.nps()
                psb = ps.bitcast(BF16)
                for c in range(8):
                    kb.op("pe", lambda e: e.transpose(out=psb[:, c * 128:(c + 1) * 128], in_=hn[i][:, c * 128:(c + 1) * 128],
                                                      identity=self.ident[:]), reads=[bh[i], self.b_ident], writes=[pb])
                kb.op("dve", lambda e: e.tensor_tensor(
                    out=self.hT[:, :, t * 128:(t + 1) * 128], in0=psb[:, :].rearrange("p (c t) -> p c t", c=8),
                    in1=self.gmix[:, :].unsqueeze(2).to_broadcast([128, 8, 128]), op=ALU.mult),
                    reads=[pb, self.b_gmix], writes=[self.b_hT[t // 4]])

            for t in range(3):
                kb.dma(xt[t][:], self.x[t * 128:(t + 1) * 128, :], writes=[bx[t]])
            for t in range(NT):
                i = t % 2
                i4 = t % 4
                if t + 3 < NT:
                    kb.dma(xt[(t + 3) % 4][:], self.x[(t + 3) * 128:(t + 4) * 128, :], writes=[bx[(t + 3) % 4]])
                self.rms(xt[i4][:], bx[i4], junk, bj, ss[i], bs[i])
                kb.op("dve", lambda e: e.tensor_scalar(out=hn[i][:], in0=xt[i4][:], scalar1=ss[i][:, 0:1], scalar2=None,
                                                       op0=ALU.mult), reads=[bx[i4], bs[i]], writes=[bh[i]])
                if t > 0:
                    tail(t - 1)
            tail(NT - 1)
            kb.barrier()

    def phase2(self):
        kb = self.kb
        hT, bhT = self.hT, self.b_hT
        with contextlib.ExitStack() as es:
            self.make_wl(es, 3, 8, 128, "p2w")
            wq = [kb.sbuf_in(es, "p2wq%d" % i, [128, 8, 128], BF16) for i in range(2)]
            wk = [kb.sbuf_in(es, "p2wk%d" % i, [128, 8, 128], BF16) for i in range(2)]
            wv = [kb.sbuf_in(es, "p2wv%d" % i, [128, 8, 128], BF16) for i in range(2)]
            bwq = [Buf() for _ in range(2)]
            bwk = [Buf() for _ in range(2)]
            bwv = [Buf() for _ in range(2)]
            V = kb.sbuf_in(es, "p2V", [128, 3, 32, 2, 65], BF16)
            bV = [Buf() for _ in range(3)]
            qT = kb.sbuf_in(es, "p2qT", [128, S], BF16)
            kT = kb.sbuf_in(es, "p2kT", [128, S], BF16)
            bq, bk = Buf(), Buf()
            acc = kb.sbuf_in(es, "p2acc", [65, 2, S], F32)
            bacc = [Buf(), Buf()]
            bias = kb.sbuf_in(es, "p2bias", [128, 3, 2, 256], F32)
            bbias = Buf()
            NSC = 3
            sc = [kb.sbuf_in(es, "p2sc%d" % i, [128, 2, 512], F32) for i in range(NSC)]
            bsc = [Buf() for _ in range(NSC)]
            pT = [kb.sbuf_in(es, "p2pT%d" % i, [128, 2, 512], BF16) for i in range(NSC)]
            bpT = [Buf() for _ in range(NSC)]
            yst = [kb.sbuf_in(es, "p2y%d" % i, [64, S], BF16) for i in range(2)]
            byst = [Buf(), Buf()]
            isc = 0
            wi = 0
            kb.op("pool", lambda e: e.memset(V[:], 1.0), writes=bV)
            units = [(pp, gg) for pp in range(4) for gg in range(3)]

            def issue(pp, gg):
                return [self.load_w_dma(self.win(off + gg * 512 + pp * 128, 128), 8, 128) for off in (OFF_VA, OFF_QA, OFF_KA)]
            pend = [issue(*units.pop(0))]
            for p in range(4):
                kb.dma(bias[:], self.biasT.rearrange("k (g h c) -> k g h c", g=3, h=8)[:, :, 2 * p:2 * p + 2, :],
                       writes=[bbias])
                for g, d in enumerate(DILS):
                    kb.tick()
                    nb = 32 // d
                    w = wi % 2
                    wi += 1
                    toks = pend.pop(0)
                    self.load_w_cast(toks[0], wv[w][:], bwv[w])
                    self.load_w_cast(toks[1], wq[w][:], bwq[w])
                    self.load_w_cast(toks[2], wk[w][:], bwk[w])
                    nxt = units.pop(0) if units else None
                    if nxt is not None:
                        pend.append(issue(*nxt))
                    for tt in range(8):
                        ps, pb = self.nps()
                        for c in range(8):
                            kb.op("pe", lambda e: e.matmul(ps[:, :], lhsT=wv[w][:, c, :], rhs=hT[:, c, tt * 512:(tt + 1) * 512],
                                                           start=(c == 0), stop=(c == 7)), reads=[bhT[tt], bwv[w]], writes=[pb])
                        eng = "act" if tt % 2 else "dve"
                        kb.op(eng, lambda e: (e.copy if eng == "act" else e.tensor_copy)(
                            out=kT[:, tt * 512:(tt + 1) * 512], in_=ps[:, :]), reads=[pb], writes=[bk])
                    for b8 in range(0, 32, 8):
                        ps, pb = self.nps()
                        psb = ps.bitcast(BF16)
                        for bi in range(8):
                            r, m = divmod(b8 + bi, nb)
                            kb.op("pe", lambda e: e.transpose(out=psb[:, bi * 128:(bi + 1) * 128], in_=kT[:, _tok(d, r, m)],
                                                              identity=self.ident[:]), reads=[bk, self.b_ident], writes=[pb])
                        eng = "act" if (b8 // 8) % 2 else "dve"
                        kb.op(eng, lambda e: (e.copy if eng == "act" else e.tensor_copy)(
                            out=V[:, g, b8:b8 + 8, :, 0:64], in_=psb[:, :].rearrange("p (b h e) -> p b h e", b=8, h=2)),
                            reads=[pb], writes=[bV[g]])
                    for tt in range(8):
                        for (wt, bw, dst, bd, eng) in ((wq[w], bwq[w], qT, bq, "act"), (wk[w], bwk[w], kT, bk, "dve")):
                            ps, pb = self.nps()
                            for c in range(8):
                                kb.op("pe", lambda e: e.matmul(ps[:, :], lhsT=wt[:, c, :], rhs=hT[:, c, tt * 512:(tt + 1) * 512],
                                                               start=(c == 0), stop=(c == 7)), reads=[bhT[tt], bw], writes=[pb])
                            kb.op(eng, lambda e: (e.copy if eng == "act" else e.tensor_copy)(
                                out=dst[:, tt * 512:(tt + 1) * 512], in_=ps[:, :]), reads=[pb], writes=[bd])
                    its = [(r, m2) for r in range(d) for m2 in range(0, nb, 2)]
                    SK = 2
                    state = {}

                    def front(it):
                        nonlocal isc
                        r, m2 = it
                        j = isc % NSC
                        isc += 1
                        pss = [self.nps() for _ in range(2)]
                        for bi in range(2):
                            m = m2 + bi
                            tq = _tok(d, r, m)
                            tp = _tok(d, r, m - 1) if m > 0 else tq
                            for part, tk in ((0, tp), (1, tq)):
                                for hh in range(2):
                                    pr = slice(hh * 64, hh * 64 + 64)
                                    psS, pbS = pss[hh]
                                    c0 = bi * 256 + part * 128
                                    kb.op("pe", lambda e: e.matmul(psS[:, c0:c0 + 128], lhsT=kT[pr, tk], rhs=qT[pr, tq],
                                                                   start=True, stop=True), reads=[bq, bk], writes=[pbS])
                        for hh in range(2):
                            psS, pbS = pss[hh]
                            kb.op("dve", lambda e: e.scalar_tensor_tensor(
                                out=sc[j][:, hh, :].rearrange("p (b c) -> p b c", b=2), in0=psS[:, :].rearrange("p (b c) -> p b c", b=2),
                                scalar=0.125, in1=bias[:, g, hh, :].unsqueeze(1).to_broadcast([128, 2, 256]),
                                op0=ALU.mult, op1=ALU.add), reads=[pbS, bbias], writes=[bsc[j]])
                        kb.op("act", lambda e: e.activation(out=pT[j][:], in_=sc[j][:], func=AF.Exp),
                              reads=[bsc[j]], writes=[bpT[j]])
                        state[it] = j

                    def back(it):
                        r, m2 = it
                        j = state.pop(it)
                        for hh in range(2):
                            psO, pbO = self.nps()
                            for bi in range(2):
                                m = m2 + bi
                                if m > 0:
                                    kb.op("pe", lambda e: e.matmul(psO[0:65, bi * 128:(bi + 1) * 128], lhsT=V[:, g, r * nb + m - 1, hh, :],
                                                                   rhs=pT[j][:, hh, bi * 256:bi * 256 + 128], start=True, stop=False),
                                          reads=[bV[g], bpT[j]], writes=[pbO])
                                kb.op("pe", lambda e: e.matmul(psO[0:65, bi * 128:(bi + 1) * 128], lhsT=V[:, g, r * nb + m, hh, :],
                                                               rhs=pT[j][:, hh, bi * 256 + 128:bi * 256 + 256], start=(m == 0), stop=True),
                                      reads=[bV[g], bpT[j]], writes=[pbO])
                            dst = acc[:, hh, _tok(d, r, m2, 2)]
                            if g == 0:
                                kb.op("act", lambda e: e.copy(out=dst, in_=psO[0:65, 0:256]), reads=[pbO], writes=[bacc[hh]])
                            else:
                                kb.op("dve", lambda e: e.tensor_tensor(out=dst, in0=dst, in1=psO[0:65, 0:256], op=ALU.add),
                                      reads=[pbO, bacc[hh]], writes=[bacc[hh]])

                    for i in range(len(its) + SK):
                        if i < len(its):
                            front(its[i])
                        if i >= SK:
                            back(its[i - SK])
                for hh in range(2):
                    kb.op("act", lambda e: e.activation(out=acc[64:65, hh, :], in_=acc[64:65, hh, :], func=AF.Ln),
                          reads=[bacc[hh]], writes=[bacc[hh]])
                    kb.op("act", lambda e: e.activation(out=acc[64:65, hh, :], in_=acc[64:65, hh, :], func=AF.Exp, scale=-1.0),
                          reads=[bacc[hh]], writes=[bacc[hh]])
                    for tt in range(8):
                        sl = slice(tt * 512, (tt + 1) * 512)
                        ps, pb = self.nps()
                        kb.op("pe", lambda e: e.matmul(ps[0:64, :], lhsT=self.ones[64:65, 0:64], rhs=acc[64:65, hh, sl],
                                                       start=True, stop=True), reads=[self.b_ones, bacc[hh]], writes=[pb])
                        kb.op("dve", lambda e: e.tensor_tensor(out=yst[hh][:, sl], in0=acc[0:64, hh, sl], in1=ps[0:64, :],
                                                               op=ALU.mult), reads=[pb, bacc[hh]], writes=[byst[hh]])
                    hd = 2 * p + hh
                    kb.dma_later(2, self.ybT_d[hd * 64:(hd + 1) * 64, :], yst[hh][:], reads=[byst[hh]])
            kb.barrier()

    def phase3a(self):
        kb = self.kb
        hT, bhT = self.hT, self.b_hT
        es3 = contextlib.ExitStack()
        self.es3 = es3
        nm = ["ig", "lf", "bt", "blast", "bias_s", "w_inter", "wk", "decay", "e_s"]
        self.gt = {n: kb.sbuf_in(es3, "g_" + n, [128, 128], F32) for n in nm}
        self.bgt = {n: Buf() for n in nm}
        gt, bgt = self.gt, self.bgt
        def gates(es):
                wif = kb.sbuf_in(es, "p3wif", [128, 8, 8], BF16)
                bwif = Buf()
                wif_st = kb.sbuf_in(es, "p3wifst", [128, 8, 8], F32)
                bwif_st = Buf()
                kb.dma(wif_st[:], self.win(OFF_IF, 8), writes=[bwif_st])
                kb.op("dve", lambda e: e.tensor_copy(out=wif[:], in_=wif_st[:]), reads=[bwif_st], writes=[bwif])
                bif = kb.sbuf_in(es, "p3bif", [128, 8], F32)
                bbif = Buf()
                kb.dma(bif[:], self.b_if[:, :], writes=[bbif])
                gsb = kb.sbuf_in(es, "p3gsb", [128, 32, 8], F32)
                bgsb = Buf()
                tmp = kb.sbuf_in(es, "p3tmp", [128, 128], F32)
                btmp = Buf()
                ps, pb = self.nps()
                for t in range(NT):
                    for c in range(8):
                        kb.op("pe", lambda e: e.matmul(ps[:, t * 8:(t + 1) * 8], lhsT=hT[:, c, t * 128:(t + 1) * 128], rhs=wif[:, c, :],
                                                       start=(c == 0), stop=(c == 7)), reads=[bhT[t // 4], bwif], writes=[pb])
                kb.op("dve", lambda e: e.tensor_tensor(out=gsb[:], in0=ps[:, 0:256].rearrange("p (t g) -> p t g", g=8),
                                                       in1=bif[:, :].unsqueeze(1).to_broadcast([128, 32, 8]), op=ALU.add),
                      reads=[pb, bbif], writes=[bgsb])
                v3 = lambda a: a[:, :].rearrange("p (t h) -> p t h", h=4)
                kb.op("dve", lambda e: e.tensor_copy(out=v3(gt["ig"]), in_=gsb[:, :, 0:4]), reads=[bgsb], writes=[bgt["ig"]])
                kb.op("act", lambda e: e.activation(out=v3(tmp), in_=gsb[:, :, 4:8], func=AF.Exp, scale=-1.0),
                      reads=[bgsb], writes=[btmp])
                kb.op("act", lambda e: e.activation(out=tmp[:], in_=tmp[:], func=AF.Ln, bias=1.0), reads=[btmp], writes=[btmp])
                kb.op("dve", lambda e: e.tensor_scalar(out=gt["lf"][:], in0=tmp[:], scalar1=-1.0, scalar2=None, op0=ALU.mult),
                      reads=[btmp], writes=[bgt["lf"]])
                ps, pb = self.nps()
                kb.op("pe", lambda e: e.matmul(ps[:, 0:128], lhsT=self.tri[:], rhs=gt["lf"][:], start=True, stop=True),
                      reads=[self.b_tri, bgt["lf"]], writes=[pb])
                kb.op("dve", lambda e: e.tensor_copy(out=gt["bt"][:], in_=ps[:, 0:128]), reads=[pb], writes=[bgt["bt"]])
                ps, pb = self.nps()
                kb.op("pe", lambda e: e.matmul(ps[:, 0:128], lhsT=self.ones[:], rhs=gt["lf"][:], start=True, stop=True),
                      reads=[self.b_ones, bgt["lf"]], writes=[pb])
                kb.op("dve", lambda e: e.tensor_copy(out=gt["blast"][:], in_=ps[:, 0:128]), reads=[pb], writes=[bgt["blast"]])
                kb.op("dve", lambda e: e.scalar_tensor_tensor(out=gt["bias_s"][:], in0=gt["ig"][:], scalar=-LN16, in1=gt["bt"][:],
                                                              op0=ALU.add, op1=ALU.subtract),
                      reads=[bgt["ig"], bgt["bt"]], writes=[bgt["bias_s"]])
                kb.op("dve", lambda e: e.tensor_tensor(out=tmp[:], in0=gt["ig"][:], in1=gt["bt"][:], op=ALU.subtract),
                      reads=[bgt["ig"], bgt["bt"]], writes=[btmp])
                kb.op("act", lambda e: e.activation(out=gt["e_s"][:], in_=tmp[:], func=AF.Exp), reads=[btmp], writes=[bgt["e_s"]])
                kb.op("act", lambda e: e.activation(out=gt["w_inter"][:], in_=gt["bt"][:], func=AF.Exp, bias=-LN16),
                      reads=[bgt["bt"]], writes=[bgt["w_inter"]])
                kb.op("dve", lambda e: e.scalar_tensor_tensor(out=tmp[:], in0=gt["bias_s"][:], scalar=LN16, in1=gt["blast"][:],
                                                              op0=ALU.add, op1=ALU.add),
                      reads=[bgt["bias_s"], bgt["blast"]], writes=[btmp])
                kb.op("act", lambda e: e.activation(out=gt["wk"][:], in_=tmp[:], func=AF.Exp), reads=[btmp], writes=[bgt["wk"]])
                kb.op("act", lambda e: e.activation(out=gt["decay"][:], in_=gt["blast"][:], func=AF.Exp),
                      reads=[bgt["blast"]], writes=[bgt["decay"]])


        with contextlib.ExitStack() as es:
            self.make_wl(es, 2, 8, 128, "p3wm")
            cw = kb.sbuf_in(es, "p3cw", [128, 16, 4], F32)
            cb = kb.sbuf_in(es, "p3cb", [128, 16], F32)
            bcw = Buf()
            kb.dma(cw[:], self.conv_w[:, :, :], writes=[bcw])
            kb.dma(cb[:], self.conv_b[:, :], writes=[bcw])
            wqk = [kb.sbuf_in(es, "p3wqk%d" % i, [128, 8, 128], BF16) for i in range(2)]
            bwqk = [Buf(), Buf()]
            pre = [kb.sbuf_in(es, "p3pre%d" % i, [128, 3 + S], F32) for i in range(2)]
            bpre = [Buf(), Buf()]
            cacc = [kb.sbuf_in(es, "p3acc%d" % i, [128, S], F32) for i in range(2)]
            bcacc = [Buf(), Buf()]
            qo = [kb.sbuf_in(es, "p3qo%d" % i, [128, S], BF16) for i in range(1)]
            bqo = [Buf()]
            wvo = [kb.sbuf_in(es, "p3wvo%d" % i, [128, 8, 512], BF16) for i in range(2)]
            bwvo = [Buf(), Buf()]
            vst = [kb.sbuf_in(es, "p3vst%d" % i, [128, 512], BF16) for i in range(2)]
            bvst = [Buf(), Buf()]
            for i in range(2):
                kb.op("pool", lambda e: e.memset(pre[i][:, 0:3], 0.0), writes=[bpre[i]])

            def qk_front(j):
                i = j % 2
                self.load_w(self.win(j * 128, 128), wqk[i][:], bwqk[i], 8, 128, eng="act")
                for tt in range(8):
                    ps, pb = self.nps()
                    for c in range(8):
                        kb.op("pe", lambda e: e.matmul(ps[:, :], lhsT=wqk[i][:, c, :], rhs=hT[:, c, tt * 512:(tt + 1) * 512],
                                                       start=(c == 0), stop=(c == 7)), reads=[bhT[tt], bwqk[i]], writes=[pb])
                    kb.op("act", lambda e: e.copy(out=pre[i][:, 3 + tt * 512:3 + (tt + 1) * 512], in_=ps[:, :]),
                          reads=[pb], writes=[bpre[i]])

            def qk_conv(j):
                i = j % 2
                kb.op("dve", lambda e: e.tensor_scalar(out=cacc[i][:], in0=pre[i][:, 0:S], scalar1=cw[:, j, 0:1], scalar2=None,
                                                       op0=ALU.mult), reads=[bpre[i], bcw], writes=[bcacc[i]])
                for tap in range(1, 4):
                    kb.op("dve", lambda e: e.scalar_tensor_tensor(out=cacc[i][:], in0=pre[i][:, tap:tap + S], scalar=cw[:, j, tap:tap + 1],
                                                                  in1=cacc[i][:], op0=ALU.mult, op1=ALU.add),
                          reads=[bpre[i], bcw, bcacc[i]], writes=[bcacc[i]])

            def qk_back(j):
                i = j % 2
                kb.op("act", lambda e: e.activation(out=qo[0][:], in_=cacc[i][:], func=AF.Silu, bias=cb[:, j:j + 1]),
                      reads=[bcacc[i], bcw], writes=[bqo[0]])
                dstd = self.qT_d if j < 8 else self.kT_d
                jj = j % 8
                kb.dma(dstd[jj * 128:(jj + 1) * 128, :], qo[0][:], reads=[bqo[0]], q="act")

            def vo_region(q4):
                for k4 in range(4):
                    c0 = (OFF_VM if q4 < 2 else OFF_OM) + (q4 % 2) * 512 + k4 * 128
                    self.load_w(self.win(c0, 128), wvo[q4 % 2][:, :, k4 * 128:(k4 + 1) * 128], bwvo[q4 % 2], 8, 128, eng="act")

            def vo_unit(q4, t):
                i = (q4 * NT + t) % 2
                r = q4 % 2
                ps, pb = self.nps()
                for c in range(8):
                    kb.op("pe", lambda e: e.matmul(ps[:, :], lhsT=hT[:, c, t * 128:(t + 1) * 128], rhs=wvo[r][:, c, :],
                                                   start=(c == 0), stop=(c == 7)), reads=[bhT[t // 4], bwvo[r]], writes=[pb])
                hs = slice((q4 % 2) * 512, (q4 % 2) * 512 + 512)
                if q4 < 2:
                    kb.op("act", lambda e: e.copy(out=vst[i][:], in_=ps[:, :]), reads=[pb], writes=[bvst[i]])
                    kb.dma(self.v_d[t * 128:(t + 1) * 128, hs], vst[i][:], reads=[bvst[i]], q="act")
                else:
                    kb.op("act", lambda e: e.activation(out=vst[i][:], in_=ps[:, :], func=AF.Sigmoid), reads=[pb], writes=[bvst[i]])
                    kb.dma(self.og_d[t * 128:(t + 1) * 128, hs], vst[i][:], reads=[bvst[i]], q="act")

            qk_front(0)
            vo_units = [(q4, t) for q4 in range(4) for t in range(NT)]
            vi = 0

            def vo_some(n):
                nonlocal vi
                for _ in range(n):
                    if vi < len(vo_units):
                        q4, t = vo_units[vi]
                        if t == 0:
                            vo_region(q4)
                        vo_unit(q4, t)
                        vi += 1

            for j in range(16):
                qk_conv(j)
                if j + 1 < 16:
                    qk_front(j + 1)
                vo_some(4)
                if j >= 1:
                    qk_back(j - 1)
                if j == 1:
                    gates(es)
                vo_some(4)
            qk_back(15)
            vo_some(len(vo_units))
        kb.barrier()

    def phase3b(self):
        kb = self.kb
        gt, bgt = self.gt, self.bgt
        with contextlib.ExitStack() as es:
            qw = [kb.sbuf_in(es, "r_q%d" % i, [128, 8, 512], BF16) for i in range(2)]
            kw_ = [kb.sbuf_in(es, "r_k%d" % i, [128, 8, 512], BF16) for i in range(2)]
            vw = [kb.sbuf_in(es, "r_v%d" % i, [128, 4, 4, 260], BF16) for i in range(2)]
            ow = [kb.sbuf_in(es, "r_o%d" % i, [128, 4, 1024], BF16) for i in range(2)]
            yaw = [kb.sbuf_in(es, "r_y%d" % i, [128, 8, 512], BF16) for i in range(2)]
            bq = [Buf(), Buf()]
            bk = [Buf(), Buf()]
            bv = [Buf(), Buf()]
            bo = [Buf(), Buf()]
            by = [Buf(), Buf()]
            C32 = kb.sbuf_in(es, "r_C32", [128, 4, 2, 257], F32)
            Cb = kb.sbuf_in(es, "r_Cb", [128, 2, 4, 2, 260], BF16)
            bC32 = [Buf() for _ in range(4)]
            bCb = [[Buf() for _ in range(4)] for _ in range(2)]
            kb.op("pool", lambda e: e.memset(C32[:], 0.0), writes=bC32)
            kb.op("pool", lambda e: e.memset(Cb[:, :, :, :, :].rearrange("p a h d e -> p (a h d e)"), 0.0), writes=bCb[0] + bCb[1])
            for i in range(2):
                kb.op("pool", lambda e: e.memset(vw[i][:], 1.0), writes=[bv[i]])

            def rot(name, shape, dt, n):
                return [kb.sbuf_in(es, "%s%d" % (name, i), shape, dt) for i in range(n)], [Buf() for _ in range(n)]
            WT, bWT = rot("r_WT", [128, 4, 128], BF16, 2)
            kws, bkws = rot("r_kws", [128, 4, 256], BF16, 2)
            num, bnum = rot("r_num", [128, 4, 257], F32, 3)
            yn, byn = rot("r_yn", [128, 4, 256], F32, 2)
            y2, by2 = rot("r_y2", [128, 1024], BF16, 2)
            st6, bst6 = rot("r_st6", [128, 4, 6], F32, 2)
            mv, bmv = rot("r_mv", [128, 4, 2], F32, 2)
            sm, bsm = rot("r_sm", [128, 3, 4], F32, 2)

            def load_window(w):
                wi = w % 2
                tsl = slice(w * 512, (w + 1) * 512)
                kb.dma(qw[wi][:], self.qT_d.rearrange("(c p) t -> p c t", p=128)[:, :, tsl], writes=[bq[wi]])
                kb.dma(kw_[wi][:], self.kT_d.rearrange("(c p) t -> p c t", p=128)[:, :, tsl], writes=[bk[wi]])
                for c4 in range(4):
                    t0 = w * 512 + c4 * 128
                    kb.dma(vw[wi][:, c4, :, 0:256], self.v_d[t0:t0 + 128, :].rearrange("p (h e) -> p h e", h=4), writes=[bv[wi]])
                kb.dma(ow[wi][:], self.og_d[w * 512:(w + 1) * 512, :].rearrange("(c p) n -> p c n", p=128), writes=[bo[wi]])

            def ln_a(c, w, wi, cc, csl, n2, n3):
                nm = num[n3]
                kb.op("dve", lambda e: e.tensor_scalar(out=sm[n2][:, 0, :], in0=nm[:, :, 256], scalar1=-1.0, scalar2=1.0,
                                                       op0=ALU.mult, op1=ALU.max), reads=[bnum[n3]], writes=[bsm[n2]])
                kb.op("dve", lambda e: e.tensor_tensor(out=sm[n2][:, 0, :], in0=sm[n2][:, 0, :], in1=nm[:, :, 256], op=ALU.max),
                      reads=[bnum[n3], bsm[n2]], writes=[bsm[n2]])
                for h in range(4):
                    kb.op("dve", lambda e: e.bn_stats(out=st6[n2][:, h, :], in_=nm[:, h, 0:256]), reads=[bnum[n3]], writes=[bst6[n2]])
                for h in range(4):
                    kb.op("dve", lambda e: e.bn_aggr(out=mv[n2][:, h, :], in_=st6[n2][:, h, :]), reads=[bst6[n2]], writes=[bmv[n2]])
                kb.op("dve", lambda e: e.tensor_tensor(out=sm[n2][:, 1, :], in0=sm[n2][:, 0, :], in1=sm[n2][:, 0, :], op=ALU.mult),
                      reads=[bsm[n2]], writes=[bsm[n2]])
                kb.op("dve", lambda e: e.scalar_tensor_tensor(out=sm[n2][:, 1, :], in0=sm[n2][:, 1, :], scalar=EPS, in1=mv[n2][:, :, 1],
                                                              op0=ALU.mult, op1=ALU.add), reads=[bsm[n2], bmv[n2]], writes=[bsm[n2]])
                kb.op("act", lambda e: e.activation(out=sm[n2][:, 2, :], in_=sm[n2][:, 1, :], func=AF.Sqrt), reads=[bsm[n2]], writes=[bsm[n2]])

            def ln_b(c, w, wi, cc, csl, n2, n3):
                nm = num[n3]
                kb.op("dve", lambda e: e.reciprocal(out=sm[n2][:, 2, :], in_=sm[n2][:, 2, :]), reads=[bsm[n2]], writes=[bsm[n2]])
                for h in range(4):
                    kb.op("dve", lambda e: e.tensor_scalar(out=yn[n2][:, h, :], in0=nm[:, h, 0:256], scalar1=mv[n2][:, h, 0:1],
                                                           scalar2=sm[n2][:, 2, h:h + 1], op0=ALU.subtract, op1=ALU.mult),
                          reads=[bnum[n3], bmv[n2], bsm[n2]], writes=[byn[n2]])
                kb.op("pool", lambda e: e.tensor_tensor(out=y2[n2][:], in0=yn[n2][:, :, :].rearrange("p h e -> p (h e)"),
                                                        in1=ow[wi][:, cc, :], op=ALU.mult), reads=[byn[n2], bo[wi]], writes=[by2[n2]])

            def ln_c(c, w, wi, cc, csl, n2, n3):
                psY, pbY = self.nps()
                psYb = psY.bitcast(BF16)
                for j in range(8):
                    kb.op("pe", lambda e: e.transpose(out=psYb[:, j * 128:(j + 1) * 128], in_=y2[n2][:, j * 128:(j + 1) * 128],
                                                      identity=self.ident[:]), reads=[by2[n2], self.b_ident], writes=[pbY])
                kb.op("act", lambda e: e.copy(out=yaw[wi][:, :, csl], in_=psYb[:, :].rearrange("p (j t) -> p j t", j=8)),
                      reads=[pbY], writes=[by[wi]])
                if cc == 3:
                    kb.dma(self.yaT_d.rearrange("(c p) t -> p c t", p=128)[:, :, w * 512:(w + 1) * 512], yaw[wi][:], reads=[by[wi]], q="act")

            def ln_tick(t):
                for fn, lag in ((ln_a, 1), (ln_b, 2), (ln_c, 3)):
                    cz = t - lag
                    if 0 <= cz < NT:
                        fn(*ln_args[cz])

            ln_args = {}
            load_window(0)
            stS = {}
            self.ps_lo = 2

            def stage1(c):
                w, cc = divmod(c, 4)
                wi = w % 2
                csl = slice(cc * 128, (cc + 1) * 128)
                psS, pbS = self.ps[0], self.pb[0]
                psK, pbK = self.ps[1], self.pb[1]
                psKb = psK.bitcast(BF16)
                for h in range(4):
                    for dc in range(2):
                        kb.op("pe", lambda e: e.matmul(psS[:, h * 128:(h + 1) * 128], lhsT=kw_[wi][:, 2 * h + dc, csl], rhs=qw[wi][:, 2 * h + dc, csl],
                                                       start=(dc == 0), stop=(dc == 1)), reads=[bk[wi], bq[wi]], writes=[pbS])
                for h in range(4):
                    for dc in range(2):
                        kb.op("pe", lambda e: e.transpose(out=psKb[:, (2 * h + dc) * 128:(2 * h + dc + 1) * 128], in_=kw_[wi][:, 2 * h + dc, csl],
                                                          identity=self.ident[:]), reads=[bk[wi], self.b_ident], writes=[pbK])
                stS[c] = (psS, pbS, psKb, pbK)

            stage1(0)
            for c in range(NT):
                w, cc = divmod(c, 4)
                wi = w % 2
                if cc == 2 and w + 1 < 8:
                    load_window(w + 1)
                csl = slice(cc * 128, (cc + 1) * 128)
                n2 = c % 2
                n3 = c % 3
                psS, pbS, psKb, pbK = stS.pop(c)
                for h in range(4):
                    ch = c * 4 + h
                    kb.op("dve", lambda e: e.scalar_tensor_tensor(out=WT[n2][:, h, :], in0=psS[:, h * 128:(h + 1) * 128],
                                                                  scalar=gt["e_s"][:, ch:ch + 1], in1=self.tri[:], op0=ALU.mult, op1=ALU.mult),
                          reads=[pbS, bgt["e_s"], self.b_tri], writes=[bWT[n2]])
                for h in range(4):
                    ch = c * 4 + h
                    kb.op("act", lambda e: e.activation(out=kws[n2][:, h, :], in_=psKb[:, h * 256:(h + 1) * 256], func=AF.Copy,
                                                        scale=gt["wk"][:, ch:ch + 1]), reads=[pbK, bgt["wk"]], writes=[bkws[n2]])
                for h in range(4):
                    ch = c * 4 + h
                    psN, pbN = self.nps()
                    kb.op("pe", lambda e: e.matmul(psN[:, 0:257], lhsT=WT[n2][:, h, :], rhs=vw[wi][:, cc, h, 0:257], start=True, stop=False),
                          reads=[bWT[n2], bv[wi]], writes=[pbN])
                    for dc in range(2):
                        kb.op("pe", lambda e: e.matmul(psN[:, 0:257], lhsT=qw[wi][:, 2 * h + dc, csl], rhs=Cb[:, c % 2, h, dc, 0:257],
                                                       start=False, stop=(dc == 1)), reads=[bq[wi], bCb[c % 2][h]], writes=[pbN])
                    kb.op("act", lambda e: e.activation(out=num[n3][:, h, :], in_=psN[:, 0:257], func=AF.Copy,
                                                        scale=gt["w_inter"][:, ch:ch + 1]), reads=[pbN, bgt["w_inter"]], writes=[bnum[n3]])
                ln_args[c] = (c, w, wi, cc, csl, n2, n3)
                ln_tick(c)
                if c < NT - 1:
                    for h in range(4):
                        ch = c * 4 + h
                        for dc in range(2):
                            psC, pbC = self.nps()
                            kb.op("pe", lambda e: e.matmul(psC[:, 0:257], lhsT=kws[n2][:, h, dc * 128:(dc + 1) * 128], rhs=vw[wi][:, cc, h, 0:257],
                                                           start=True, stop=True), reads=[bkws[n2], bv[wi]], writes=[pbC])
                            kb.op("dve", lambda e: e.scalar_tensor_tensor(out=C32[:, h, dc, :], in0=C32[:, h, dc, :],
                                                                          scalar=gt["decay"][:, ch:ch + 1], in1=psC[:, 0:257],
                                                                          op0=ALU.mult, op1=ALU.add),
                                  reads=[pbC, bgt["decay"], bC32[h]], writes=[bC32[h]])
                        kb.op("act", lambda e: e.copy(out=Cb[:, (c + 1) % 2, h, :, 0:257], in_=C32[:, h, :, :]), reads=[bC32[h]],
                              writes=[bCb[(c + 1) % 2][h]])
                if c + 1 < NT:
                    stage1(c + 1)
            for t in range(NT, NT + 3):
                ln_tick(t)
        self.ps_lo = 0
        self.es3.close()
        self.es3 = None
        kb.barrier()

    def phase4(self):
        kb = self.kb
        hT, bhT = self.hT, self.b_hT
        with contextlib.ExitStack() as es:
            self.make_wl(es, 2, 8, 256, "p4w")
            Wa = kb.sbuf_in(es, "p4Wa", [128, 8, 1024], BF16)
            Wb = kb.sbuf_in(es, "p4Wb", [128, 4, 1024], BF16)
            Wg = kb.sbuf_in(es, "p4Wg", [128, 8, 2048], BF16)
            ya = [kb.sbuf_in(es, "p4ya%d" % i, [128, 8, 512], BF16) for i in range(2)]
            yb = [kb.sbuf_in(es, "p4yb%d" % i, [128, 4, 512], BF16) for i in range(2)]
            bya, byb = [Buf(), Buf()], [Buf(), Buf()]
            for T0 in range(2):
                kb.dma(ya[T0][:], self.yaT_d.rearrange("(c p) t -> p c t", p=128)[:, :, T0 * 512:(T0 + 1) * 512], writes=[bya[T0]])
                kb.dma(yb[T0][:], self.ybT_d.rearrange("(c p) t -> p c t", p=128)[:, :, T0 * 512:(T0 + 1) * 512], writes=[byb[T0]])
            bWa = [Buf() for _ in range(4)]
            bWb = [Buf() for _ in range(4)]
            bWg = [Buf() for _ in range(8)]
            def load_set(q):
                cs = slice(q * 256, (q + 1) * 256)
                self.load_w(self.w_a[:, cs].rearrange("(c p) n -> p c n", p=128), Wa[:, :, cs], bWa[q], 8, 256, self.gml, self.b_gml)
                self.load_w(self.w_b[:, cs].rearrange("(c p) n -> p c n", p=128), Wb[:, :, cs], bWb[q], 4, 256)
                for q2 in (q, q + 4):
                    self.load_w(self.win(OFF_GATE + q2 * 256, 256), Wg[:, :, q2 * 256:(q2 + 1) * 256], bWg[q2], 8, 256)
            mT = [kb.sbuf_in(es, "p4mT%d" % i, [128, 8, 512], BF16) for i in range(2)]
            bmT = [Buf(), Buf()]
            sa = [kb.sbuf_in(es, "p4sa%d" % i, [128, 512], F32) for i in range(2)]
            sb_ = [kb.sbuf_in(es, "p4sb%d" % i, [128, 512], F32) for i in range(2)]
            bsa, bsb = [Buf(), Buf()], [Buf(), Buf()]
            k2 = 0

            def ld_y(T):
                ti = T % 2
                tsl = slice(T * 512, (T + 1) * 512)
                kb.dma(ya[ti][:], self.yaT_d.rearrange("(c p) t -> p c t", p=128)[:, :, tsl], writes=[bya[ti]])
                kb.dma(yb[ti][:], self.ybT_d.rearrange("(c p) t -> p c t", p=128)[:, :, tsl], writes=[byb[ti]])
            def unit(T, fc):
                nonlocal k2
                ti = T % 2
                tsl = slice(T * 512, (T + 1) * 512)
                fs = slice(fc * 128, (fc + 1) * 128)
                psA, pbA = self.nps()
                for c in range(8):
                    kb.op("pe", lambda e: e.matmul(psA[:, :], lhsT=Wa[:, c, fs], rhs=ya[ti][:, c, :], start=(c == 0), stop=(c == 7)),
                          reads=[bWa[fc // 2], bya[ti]], writes=[pbA])
                psB, pbB = self.nps()
                for c in range(4):
                    kb.op("pe", lambda e: e.matmul(psB[:, :], lhsT=Wb[:, c, fs], rhs=yb[ti][:, c, :], start=(c == 0), stop=(c == 3)),
                          reads=[bWb[fc // 2], byb[ti]], writes=[pbB])
                psGA, pbGA = self.nps()
                for c in range(8):
                    kb.op("pe", lambda e: e.matmul(psGA[:, :], lhsT=Wg[:, c, fs], rhs=hT[:, c, tsl], start=(c == 0), stop=(c == 7)),
                          reads=[bWg[fc // 2], bhT[T]], writes=[pbGA])
                psGB, pbGB = self.nps()
                for c in range(8):
                    kb.op("pe", lambda e: e.matmul(psGB[:, :], lhsT=Wg[:, c, 1024 + fc * 128:1024 + (fc + 1) * 128], rhs=hT[:, c, tsl],
                                                   start=(c == 0), stop=(c == 7)), reads=[bWg[4 + fc // 2], bhT[T]], writes=[pbGB])
                i2 = k2 % 2
                k2 += 1
                kb.op("act", lambda e: e.activation(out=sa[i2][:], in_=psGA[:, :], func=AF.Sigmoid), reads=[pbGA], writes=[bsa[i2]])
                kb.op("act", lambda e: e.activation(out=sb_[i2][:], in_=psGB[:, :], func=AF.Sigmoid), reads=[pbGB], writes=[bsb[i2]])
                kb.op("dve", lambda e: e.tensor_tensor(out=sa[i2][:], in0=sa[i2][:], in1=psA[:, :], op=ALU.mult),
                      reads=[bsa[i2], pbA], writes=[bsa[i2]])
                kb.op("dve", lambda e: e.tensor_tensor(out=sb_[i2][:], in0=sb_[i2][:], in1=psB[:, :], op=ALU.mult),
                      reads=[bsb[i2], pbB], writes=[bsb[i2]])
                kb.op("dve", lambda e: e.tensor_tensor(out=mT[ti][:, fc, :], in0=sa[i2][:], in1=sb_[i2][:], op=ALU.add),
                      reads=[bsa[i2], bsb[i2]], writes=[bmT[ti]])

            def store(T):
                ti = T % 2
                kb.tick()
                kb.dma_later(1, self.mT_d.rearrange("(c p) t -> p c t", p=128)[:, :, T * 512:(T + 1) * 512], mT[ti][:], reads=[bmT[ti]])

            for fc in range(8):
                if fc % 2 == 0:
                    load_set(fc // 2)
                unit(0, fc)
                unit(1, fc)
            store(0)
            store(1)
            for T in range(2, 8):
                ld_y(T)
                for fc in range(8):
                    unit(T, fc)
                store(T)
            kb.barrier()

    def phase4b(self):
        kb = self.kb
        with contextlib.ExitStack() as es:
            self.make_wl(es, 2, 8, 256, "p4bw")
            Wo = kb.sbuf_in(es, "p4Wo", [128, 8, 1024], BF16)
            bWo = [Buf() for _ in range(4)]
            mT = [kb.sbuf_in(es, "p4bm%d" % i, [128, 8, 512], BF16) for i in range(2)]
            bmT = [Buf(), Buf()]
            hfs = [kb.sbuf_in(es, "p4hf%d" % i, [128, 8, 512], BF16) for i in range(2)]
            bhfs = [Buf(), Buf()]
            NB = 3
            xt = [kb.sbuf_in(es, "p4x%d" % i, [128, D], F32) for i in range(NB)]
            bx = [Buf() for _ in range(NB)]
            x1 = [kb.sbuf_in(es, "p4x1%d" % i, [128, D], F32) for i in range(NB)]
            bx1 = [Buf() for _ in range(NB)]
            junk = kb.sbuf_in(es, "p4j", [128, D], BF16)
            bj = Buf()
            ss = [kb.sbuf_in(es, "p4s%d" % i, [128, 1], F32) for i in range(NB)]
            bs = [Buf() for _ in range(NB)]
            hn = [kb.sbuf_in(es, "p4h%d" % i, [128, D], BF16) for i in range(NB)]
            bh = [Buf() for _ in range(NB)]

            def front(t):
                T, u = divmod(t, 4)
                ti = T % 2
                i = t % NB
                if u == 0 and t > 0:
                    kb.dma(mT[ti][:], self.mT_d.rearrange("(c p) t -> p c t", p=128)[:, :, T * 512:(T + 1) * 512], writes=[bmT[ti]])
                usl = slice(u * 128, (u + 1) * 128)
                if t > 0:
                    kb.dma(xt[i][:], self.x[t * 128:(t + 1) * 128, :], writes=[bx[i]])
                for half in range(2):
                    hs = slice(half * 512, (half + 1) * 512)
                    ps, pb = self.nps()
                    for c in range(8):
                        kb.op("pe", lambda e: e.matmul(ps[:, :], lhsT=mT[ti][:, c, usl], rhs=Wo[:, c, hs], start=(c == 0), stop=(c == 7)),
                              reads=[bmT[ti], bWo[2 * half], bWo[2 * half + 1]], writes=[pb])
                    kb.op("dve", lambda e: e.tensor_tensor(out=x1[i][:, hs], in0=xt[i][:, hs], in1=ps[:, :], op=ALU.add),
                          reads=[bx[i], pb], writes=[bx1[i]])
                kb.tick()
                kb.dma_later(1, self.x1_d[t * 128:(t + 1) * 128, :], x1[i][:], reads=[bx1[i]])
                self.rms(x1[i][:], bx1[i], junk, bj, ss[i], bs[i])
                kb.op("dve", lambda e: e.tensor_scalar(out=hn[i][:], in0=x1[i][:], scalar1=ss[i][:, 0:1], scalar2=None,
                                                       op0=ALU.mult), reads=[bx1[i], bs[i]], writes=[bh[i]])

            def back(t):
                T, u = divmod(t, 4)
                ti = T % 2
                i = t % NB
                usl = slice(u * 128, (u + 1) * 128)
                ps, pb = self.nps()
                psb = ps.bitcast(BF16)
                for c in range(8):
                    kb.op("pe", lambda e: e.transpose(out=psb[:, c * 128:(c + 1) * 128], in_=hn[i][:, c * 128:(c + 1) * 128],
                                                      identity=self.ident[:]), reads=[bh[i], self.b_ident], writes=[pb])
                kb.op("dve", lambda e: e.tensor_tensor(out=hfs[ti][:, :, usl], in0=psb[:, :].rearrange("p (c t) -> p c t", c=8),
                                                       in1=self.gffn[:, :].unsqueeze(2).to_broadcast([128, 8, 128]), op=ALU.mult),
                      reads=[pb, self.b_gffn], writes=[bhfs[ti]])
                if u == 3:
                    kb.dma_later(2, self.hfT_d.rearrange("(c p) t -> p c t", p=128)[:, :, T * 512:(T + 1) * 512], hfs[ti][:], reads=[bhfs[ti]])

            kb.dma(mT[0][:], self.mT_d.rearrange("(c p) t -> p c t", p=128)[:, :, 0:512], writes=[bmT[0]])
            kb.dma(xt[0][:], self.x[0:128, :], writes=[bx[0]])
            for q in range(4):
                cs = slice(q * 256, (q + 1) * 256)
                self.load_w(self.w_o[:, cs].rearrange("(c p) n -> p c n", p=128), Wo[:, :, cs], bWo[q], 8, 256)
            front(0)
            for t in range(NT):
                if t + 1 < NT:
                    front(t + 1)
                back(t)
            kb.barrier()

    def phase5(self):
        kb = self.kb
        with contextlib.ExitStack() as es:
            self.make_wl(es, 3, 8, 256, "p5w")
            Wd = kb.sbuf_in(es, "p5Wd", [128, NFF, 1024], BF16)
            bWd = [Buf() for _ in range(NFF)]
            dst_ = [kb.sbuf_in(es, "p5ds%d" % i, [128, 1024], F32) for i in range(2)]
            bds = [Buf(), Buf()]
            def load_wd(f):
                i = f % 2
                kb.dma(dst_[i][:], self.w_down[f * 128:(f + 1) * 128, :], writes=[bds[i]])
                kb.op("act" if f % 2 else "dve", lambda e: (e.copy if f % 2 else e.tensor_copy)(out=Wd[:, f, :], in_=dst_[i][:]),
                      reads=[bds[i]], writes=[bWd[f]])
            gfin = kb.sbuf_in(es, "p5gf", [128, 1024], F32)
            bgfin = Buf()
            kb.dma(gfin[:], self.g_fin[:, :], writes=[bgfin])
            hf = [kb.sbuf_in(es, "p5hf%d" % i, [128, 8, 1024], BF16) for i in range(2)]
            bhf = [Buf(), Buf()]
            aT = kb.sbuf_in(es, "p5aT", [128, NFF, 1024], BF16)
            baT = [Buf() for _ in range(NFF)]
            wg = [kb.sbuf_in(es, "p5wg%d" % i, [128, 8, 256], BF16) for i in range(2)]
            wu = [kb.sbuf_in(es, "p5wu%d" % i, [128, 8, 256], BF16) for i in range(2)]
            bwg, bwu = [Buf(), Buf()], [Buf(), Buf()]
            sg = [kb.sbuf_in(es, "p5sg%d" % i, [128, 512], F32) for i in range(2)]
            bsg = [Buf(), Buf()]
            xt = [kb.sbuf_in(es, "p5x%d" % i, [128, D], F32) for i in range(2)]
            bx = [Buf(), Buf()]
            x2 = [kb.sbuf_in(es, "p5x2%d" % i, [128, D], F32) for i in range(2)]
            bx2 = [Buf(), Buf()]
            junk = kb.sbuf_in(es, "p5j", [128, D], BF16)
            bj = Buf()
            ss = [kb.sbuf_in(es, "p5s%d" % i, [128, 1], F32) for i in range(2)]
            bs = [Buf(), Buf()]
            ot = [kb.sbuf_in(es, "p5o%d" % i, [128, D], F32) for i in range(2)]
            bot = [Buf(), Buf()]
            k2 = 0
            wi = 0
            for ST in range(4):
                si = ST % 2
                ssl = slice(ST * 1024, (ST + 1) * 1024)
                if ST == 0:
                    kb.dma(hf[0][:], self.hfT_d.rearrange("(c p) t -> p c t", p=128)[:, :, 0:1024], writes=[bhf[0]])
                if ST + 1 < 4:
                    kb.dma(hf[1 - si][:], self.hfT_d.rearrange("(c p) t -> p c t", p=128)[:, :, (ST + 1) * 1024:(ST + 2) * 1024],
                           writes=[bhf[1 - si]])
                for f2 in range(NFF // 2):
                    w = wi % 2
                    wi += 1
                    cs = slice(f2 * 256, (f2 + 1) * 256)
                    self.load_w(self.w_gate[:, cs].rearrange("(c p) n -> p c n", p=128), wg[w][:], bwg[w], 8, 256)
                    self.load_w(self.w_up[:, cs].rearrange("(c p) n -> p c n", p=128), wu[w][:], bwu[w], 8, 256)
                    for fi in range(2):
                        f = f2 * 2 + fi
                        fs = slice(fi * 128, (fi + 1) * 128)
                        for tt in range(2):
                            ts_ = slice(tt * 512, (tt + 1) * 512)
                            psG, pbG = self.nps()
                            for c in range(8):
                                kb.op("pe", lambda e: e.matmul(psG[:, :], lhsT=wg[w][:, c, fs], rhs=hf[si][:, c, ts_], start=(c == 0), stop=(c == 7)),
                                      reads=[bwg[w], bhf[si]], writes=[pbG])
                            psU, pbU = self.nps()
                            for c in range(8):
                                kb.op("pe", lambda e: e.matmul(psU[:, :], lhsT=wu[w][:, c, fs], rhs=hf[si][:, c, ts_], start=(c == 0), stop=(c == 7)),
                                      reads=[bwu[w], bhf[si]], writes=[pbU])
                            i2 = k2 % 2
                            k2 += 1
                            kb.op("act", lambda e: e.activation(out=sg[i2][:], in_=psG[:, :], func=AF.Silu), reads=[pbG], writes=[bsg[i2]])
                            kb.op("dve", lambda e: e.tensor_tensor(out=aT[:, f, ts_], in0=sg[i2][:], in1=psU[:, :], op=ALU.mult),
                                  reads=[bsg[i2], pbU], writes=[baT[f]])
                    if ST == 0:
                        load_wd(2 * f2)
                        load_wd(2 * f2 + 1)
                kb.dma(xt[(ST * 8) % 2][:], self.x1_d[ST * 1024:ST * 1024 + 128, :], writes=[bx[(ST * 8) % 2]])
                for u in range(8):
                    t = ST * 8 + u
                    i = t % 2
                    usl = slice(u * 128, (u + 1) * 128)
                    if u + 1 < 8:
                        kb.dma(xt[1 - i][:], self.x1_d[(t + 1) * 128:(t + 2) * 128, :], writes=[bx[1 - i]])
                    for half in range(2):
                        hs = slice(half * 512, (half + 1) * 512)
                        ps, pb = self.nps()
                        for f in range(NFF):
                            kb.op("pe", lambda e: e.matmul(ps[:, :], lhsT=aT[:, f, usl], rhs=Wd[:, f, hs], start=(f == 0), stop=(f == NFF - 1)),
                                  reads=[baT[f], bWd[f]], writes=[pb])
                        kb.op("dve", lambda e: e.tensor_tensor(out=x2[i][:, hs], in0=xt[i][:, hs], in1=ps[:, :], op=ALU.add),
                              reads=[bx[i], pb], writes=[bx2[i]])
                    self.rms(x2[i][:], bx2[i], junk, bj, ss[i], bs[i])
                    kb.op("dve", lambda e: e.tensor_tensor(out=x2[i][:], in0=x2[i][:], in1=gfin[:], op=ALU.mult),
                          reads=[bx2[i], bgfin], writes=[bx2[i]])
                    kb.op("act", lambda e: e.activation(out=ot[i][:], in_=x2[i][:], func=AF.Copy, scale=ss[i][:, 0:1]),
                          reads=[bx2[i], bs[i]], writes=[bot[i]])
                    kb.dma(self.y[t * 128:(t + 1) * 128, :], ot[i][:], reads=[bot[i]], q="act")
            kb.barrier()


def _rel_bucket(n):
    n = np.asarray(n, dtype=np.int64)
    nf = np.maximum(n, 1).astype(np.float32)
    large = 16 + (np.log(nf / np.float32(16.0)) / np.float32(np.log(128.0)) * np.float32(16.0)).astype(np.int32)
    large = np.minimum(large, 31)
    return np.where(n < 16, n, large)


def _bias_tables(rel_bias):
    out = np.full((128, 3, 8, 2, 128), -30000.0, np.float32)
    ik = np.arange(128)[:, None]
    iq = np.arange(128)[None, :]
    for g, d in enumerate(DILS):
        for j in range(2):
            dist = (128 + iq) - (ik + 128 * j)
            valid = (dist >= 0) & (dist <= 128)
            bucket = _rel_bucket(np.maximum(dist, 0) * d)
            for h in range(8):
                vals = rel_bias[bucket, g * 8 + h]
                out[:, g, h, j, :] = np.where(valid, vals, np.float32(-30000.0))
    return np.ascontiguousarray(out.reshape(128, 3 * 8 * 256))


def make_in_maps(inp):
    f = lambda a: np.ascontiguousarray(np.asarray(a, dtype=np.float32))
    pk = lambda v: f(np.asarray(v).reshape(-1, 128).T)
    rep = lambda v: f(np.tile(np.asarray(v).reshape(1, -1), (128, 1)))
    s_ = np.arange(128)
    tri = (s_[:, None] <= s_[None, :]).astype(np.float32)
    shared = {
        "w_in": f(inp["w_in"][0]),
        "g_mix": pk(inp["norm_mix_g"][0]),
        "b_if": rep(inp["b_gate_if"][0]),
        "conv_w": f(np.asarray(inp["conv_w"][0]).T.reshape(16, 128, 4).transpose(1, 0, 2)),
        "conv_b": pk(inp["conv_b"][0]),
        "g_ml": pk(inp["mlstm_norm_g"][0]),
        "w_a": f(inp["w_proj_a"][0]), "w_b": f(inp["w_proj_b"][0]), "w_o": f(inp["w_out"][0]),
        "g_ffn": pk(inp["norm_ffn_g"][0]),
        "w_gate": f(inp["w_gate"][0]), "w_up": f(inp["w_up"][0]), "w_down": f(inp["w_down"][0]),
        "g_fin": rep(inp["norm_final_g"]),
        "biasT": _bias_tables(np.asarray(inp["rel_bias"], dtype=np.float32)),
        "c_ident": np.eye(128, dtype=np.float32).astype(ml_dtypes.bfloat16),
        "c_identf": np.eye(128, dtype=np.float32),
        "c_tri": tri,
        "c_negm": np.where(tri > 0, 0.0, -30000.0).astype(np.float32),
        "c_ones": np.ones((128, 128), np.float32),
    }
    x = np.asarray(inp["x"], dtype=np.float32)
    return [dict(shared, x=np.ascontiguousarray(x[b])) for b in range(x.shape[0])]


_PROG = None


def kernel(**inputs):
    global _PROG
    if _PROG is None:
        _PROG = Prog()
    in_maps = make_in_maps(inputs)
    res = run_bass_kernel_spmd(_PROG.nc, in_maps, core_ids=list(range(8)))
    return np.stack([np.asarray(r["y"], dtype=np.float32) for r in res.results], axis=0)
```

```python
import contextlib
import numpy as np
import ml_dtypes
import concourse.bass as bass
import concourse.mybir as mybir
from concourse.bass_utils import run_bass_kernel_spmd

F32 = mybir.dt.float32
BF16 = mybir.dt.bfloat16
AF = mybir.ActivationFunctionType
ALU = mybir.AluOpType
AX = mybir.AxisListType

S = 4096
D = 1024
NT = S // 128
EPS = 1e-6
N_DMA_SEMS = 24


class Buf:
    __slots__ = ("name", "w", "r")

    def __init__(self, name=""):
        self.name = name
        self.w = None
        self.r = {}


class KB:
    def __init__(self, nc, es):
        self.nc = nc
        self.es = es
        self.eng = {"pe": nc.tensor, "act": nc.scalar, "dve": nc.vector,
                    "pool": nc.gpsimd, "sp": nc.sync}
        self.sem = {}
        self.cnt = {}
        for e in self.eng:
            self.sem[e] = es.enter_context(nc.semaphore("s_" + e))
            self.cnt[e] = 0
        for i in range(N_DMA_SEMS):
            k = ("dma", i)
            self.sem[k] = es.enter_context(nc.semaphore("s_dma%d" % i))
            self.cnt[k] = 0
        self.waited = {e: {} for e in self.eng}
        self.rr = 0
        self.n_ins = 0
        self.deferred = []

    def sbuf(self, name, shape, dt):
        return self.es.enter_context(self.nc.sbuf_tensor(name, list(shape), dt))

    def sbuf_in(self, es, name, shape, dt):
        return es.enter_context(self.nc.sbuf_tensor(name, list(shape), dt))

    def psum(self, name, shape, dt):
        return self.es.enter_context(self.nc.psum_tensor(name, list(shape), dt))

    def _deps(self, e, reads, writes):
        deps = {}

        def add(k, v):
            if deps.get(k, 0) < v:
                deps[k] = v
        for b in reads:
            if b.w is not None:
                add(*b.w)
        for b in writes:
            if b.w is not None and b.w[0] != e:
                add(*b.w)
            for k, v in b.r.items():
                if k != e:
                    add(k, v)
        if e == "pe":
            deps.pop("pe", None)
        return deps

    def _wait(self, e, deps):
        eng = self.eng[e]
        w = self.waited[e]
        for k, v in deps.items():
            if w.get(k, 0) >= v:
                continue
            eng.wait_ge(self.sem[k], v)
            w[k] = v

    def op(self, e, fn, reads=(), writes=()):
        self._wait(e, self._deps(e, reads, writes))
        ins = fn(self.eng[e])
        self.cnt[e] += 1
        c = self.cnt[e]
        ins.then_inc(self.sem[e], 1)
        for b in reads:
            b.r[e] = c
        for b in writes:
            b.w = (e, c)
            b.r = {}
        self.n_ins += 1
        return ins

    def dma(self, out, in_, reads=(), writes=(), q="sp"):
        k = ("dma", self.rr)
        self.rr = (self.rr + 1) % N_DMA_SEMS
        deps = self._deps(q, reads, writes)
        if self.cnt[k] > 0 and deps.get(k, 0) < self.cnt[k]:
            deps[k] = self.cnt[k]
        self._wait(q, deps)
        ins = self.eng[q].dma_start(out=out, in_=in_)
        self.cnt[k] += 16
        c = self.cnt[k]
        ins.then_inc(self.sem[k], 16)
        for b in reads:
            b.r[k] = c
        for b in writes:
            b.w = (k, c)
            b.r = {}
        self.n_ins += 1
        return ins

    def dma_later(self, lag, out, in_, reads=(), q="sp"):
        self.deferred.append([lag, out, in_, list(reads), q])

    def tick(self):
        keep = []
        for d in self.deferred:
            d[0] -= 1
            if d[0] <= 0:
                self.dma(d[1], d[2], reads=d[3], q=d[4])
            else:
                keep.append(d)
        self.deferred = keep

    def flush(self):
        for d in self.deferred:
            self.dma(d[1], d[2], reads=d[3], q=d[4])
        self.deferred = []

    def finish(self):
        for i in range(N_DMA_SEMS):
            k = ("dma", i)
            if self.cnt[k] > 0:
                self.eng["sp"].wait_ge(self.sem[k], self.cnt[k])
        for e in ("pe", "act", "dve", "pool"):
            if self.cnt[e] > 0:
                self.eng["sp"].wait_ge(self.sem[e], self.cnt[e])

    def barrier(self):
        self.flush()
        keys = list(self.sem.keys())
        for e in self.eng:
            w = self.waited[e]
            for k in keys:
                if k == e:
                    continue
                if self.cnt[k] > w.get(k, 0):
                    self.eng[e].wait_ge(self.sem[k], self.cnt[k])
                    w[k] = self.cnt[k]


OFF_QM, OFF_KM, OFF_VM, OFF_IF, OFF_OM, OFF_QA, OFF_KA, OFF_VA, OFF_GATE = (
    0, 1024, 2048, 3072, 3080, 4104, 5640, 7176, 8712)
IN_W = 10760
DFF = 2816
NFF = DFF // 128
LN16 = float(np.log(16.0))
DILS = (1, 4, 16)


def _tok(d, r, m, n=1):
    st = r + d * 128 * m
    return slice(st, st + d * (128 * n - 1) + 1, d)


class Prog:
    def __init__(self, dbg=False, stop_after=99):
        self.dbg = dbg
        self.stop_after = stop_after
        nc = bass.Bass("TRN2", target_bir_lowering=False)
        self.nc = nc
        self.inp = {}
        self.out = {}

        def din(name, shape, dt=F32):
            self.inp[name] = nc.dram_tensor(name, list(shape), dt, kind="ExternalInput").ap()
            return self.inp[name]

        def dscr(name, shape, dt):
            kind = "ExternalOutput" if dbg else "Internal"
            t = nc.dram_tensor(name, list(shape), dt, kind=kind).ap()
            if dbg:
                self.out[name] = t
            return t

        self.x = din("x", [S, D])
        self.w_in = din("w_in", [D, IN_W])
        self.g_mix = din("g_mix", [128, 8])
        self.b_if = din("b_if", [128, 8])
        self.conv_w = din("conv_w", [128, 16, 4])
        self.conv_b = din("conv_b", [128, 16])
        self.g_ml = din("g_ml", [128, 8])
        self.w_a = din("w_a", [D, D])
        self.w_b = din("w_b", [512, D])
        self.w_o = din("w_o", [D, D])
        self.g_ffn = din("g_ffn", [128, 8])
        self.w_gate = din("w_gate", [D, DFF])
        self.w_up = din("w_up", [D, DFF])
        self.w_down = din("w_down", [DFF, D])
        self.g_fin = din("g_fin", [128, 1024])
        self.biasT = din("biasT", [128, 3 * 8 * 256])
        self.c_ident = din("c_ident", [128, 128], BF16)
        self.c_identf = din("c_identf", [128, 128])
        self.c_tri = din("c_tri", [128, 128])
        self.c_negm = din("c_negm", [128, 128])
        self.c_ones = din("c_ones", [128, 128])

        self.y = nc.dram_tensor("y", [S, D], F32, kind="ExternalOutput").ap()
        self.ybT_d = dscr("ybT_d", [512, S], BF16)
        self.qT_d = dscr("qT_d", [1024, S], BF16)
        self.kT_d = dscr("kT_d", [1024, S], BF16)
        self.v_d = dscr("v_d", [S, 1024], BF16)
        self.og_d = dscr("og_d", [S, 1024], BF16)
        self.yaT_d = dscr("yaT_d", [1024, S], BF16)
        self.mT_d = dscr("mT_d", [1024, S], BF16)
        self.x1_d = dscr("x1_d", [S, D], F32)
        self.hfT_d = dscr("hfT_d", [1024, S], BF16)

        with contextlib.ExitStack() as es:
            kb = KB(nc, es)
            self.kb = kb
            self.ps = [kb.psum("ps%d" % i, [128, 512], F32) for i in range(8)]
            self.pb = [Buf("ps%d" % i) for i in range(8)]
            self.psi = 0
            self.consts(es)
            with contextlib.ExitStack() as es_mix:
                self.hT = kb.sbuf_in(es_mix, "hT", [128, 8, S], BF16)
                self.b_hT = [Buf("hT%d" % i) for i in range(8)]
                self.scoped(self.phase1, "phase1")
                if stop_after >= 2:
                    self.scoped(self.phase2, "phase2")
                if stop_after >= 3:
                    self.scoped(self.phase3a, "phase3a")
                if stop_after >= 4:
                    self.scoped(self.phase3b, "phase3b")
                if stop_after >= 5:
                    self.scoped(self.phase4, "phase4")
                if getattr(self, "es3", None) is not None:
                    self.es3.close()
                kb.barrier()
            if stop_after >= 5:
                self.scoped(self.phase4b, "phase4b")
            if stop_after >= 6:
                self.scoped(self.phase5, "phase5")
            kb.barrier()
            kb.finish()

    def scoped(self, fn, name):
        with self.nc.named_scope(name):
            fn()

    def nps(self):
        lo = getattr(self, "ps_lo", 0)
        i = self.psi
        if i < lo:
            i = lo
        self.psi = i + 1 if i + 1 < 8 else lo
        return self.ps[i], self.pb[i]

    def consts(self, es):
        kb = self.kb

        def ld(name, src, shape, dt=F32):
            t = kb.sbuf(name, shape, dt)
            b = Buf(name)
            kb.dma(t[:], src, writes=[b])
            return t, b
        self.ident, self.b_ident = ld("ident", self.c_ident[:, :], [128, 128], BF16)
        self.identf, self.b_identf = ld("identf", self.c_identf[:, :], [128, 128])
        self.tri, self.b_tri = ld("tri", self.c_tri[:, :], [128, 128])
        self.negm, self.b_negm = ld("negm", self.c_negm[:, :], [128, 128])
        self.ones, self.b_ones = ld("ones", self.c_ones[:, :], [128, 128])
        self.gmix, self.b_gmix = ld("gmix", self.g_mix[:, :], [128, 8])
        self.gffn, self.b_gffn = ld("gffn", self.g_ffn[:, :], [128, 8])
        self.gml, self.b_gml = ld("gml", self.g_ml[:, :], [128, 8])

    def make_wl(self, es, nbuf, kc, n, name):
        kb = self.kb
        self.wl_st = [kb.sbuf_in(es, "%s_st%d" % (name, i), [128, kc, n], F32) for i in range(nbuf)]
        self.wl_b = [Buf() for _ in range(nbuf)]
        self.wl_i = 0

    def load_w_dma(self, src, kc, n):
        i = self.wl_i
        self.wl_i = (i + 1) % len(self.wl_st)
        st, sb = self.wl_st[i], self.wl_b[i]
        self.kb.dma(st[:, 0:kc, 0:n], src, writes=[sb])
        return (st, sb, kc, n)

    def load_w_cast(self, tok, dst, dst_buf, gsc=None, gsc_buf=None, eng=None):
        kb = self.kb
        st, sb, kc, n = tok
        if eng is None:
            self.wl_e = 1 - getattr(self, "wl_e", 0)
            eng = "act" if self.wl_e else "dve"
        if gsc is None:
            kb.op(eng, lambda e: (e.copy if eng == "act" else e.tensor_copy)(out=dst, in_=st[:, 0:kc, 0:n]),
                  reads=[sb], writes=[dst_buf])
        else:
            for c in range(kc):
                if eng == "act":
                    kb.op(eng, lambda e: e.activation(out=dst[:, c, :], in_=st[:, c, 0:n], func=AF.Copy,
                                                      scale=gsc[:, c:c + 1]),
                          reads=[sb, gsc_buf], writes=[dst_buf])
                else:
                    kb.op(eng, lambda e: e.tensor_scalar(out=dst[:, c, :], in0=st[:, c, 0:n], scalar1=gsc[:, c:c + 1],
                                                         scalar2=None, op0=ALU.mult),
                          reads=[sb, gsc_buf], writes=[dst_buf])

    def load_w(self, src, dst, dst_buf, kc, n, gsc=None, gsc_buf=None, eng=None):
        self.load_w_cast(self.load_w_dma(src, kc, n), dst, dst_buf, gsc, gsc_buf, eng)

    def win(self, c0, n):
        return self.w_in[:, c0:c0 + n].rearrange("(c p) n -> p c n", p=128)

    def rms(self, xt, b_xt, junk, b_junk, ss, b_ss):
        kb = self.kb
        kb.op("act", lambda e: e.activation(out=junk[:], in_=xt, func=AF.Square, scale=float(D ** -0.5),
                                            accum_out=ss[:]), reads=[b_xt], writes=[b_junk, b_ss])
        kb.op("act", lambda e: e.activation(out=ss[:], in_=ss[:], func=AF.Sqrt, bias=EPS), reads=[b_ss], writes=[b_ss])
        kb.op("dve", lambda e: e.reciprocal(out=ss[:], in_=ss[:]), reads=[b_ss], writes=[b_ss])

    def phase1(self):
        kb = self.kb
        with contextlib.ExitStack() as es:
            xt = [kb.sbuf_in(es, "p1x%d" % i, [128, D], F32) for i in range(4)]
            bx = [Buf() for _ in range(4)]
            junk = kb.sbuf_in(es, "p1j", [128, D], BF16)
            bj = Buf()
            ss = [kb.sbuf_in(es, "p1s%d" % i, [128, 1], F32) for i in range(2)]
            bs = [Buf() for _ in range(2)]
            hn = [kb.sbuf_in(es, "p1h%d" % i, [128, D], BF16) for i in range(2)]
            bh = [Buf() for _ in range(2)]
            def tail(t):
                i = t % 2
                ps, pb = self.nps()
                psb = ps.bitcast(BF16)
                for c in range(8):
                    kb.op("pe", lambda e: e.transpose(out=psb[:, c * 128:(c + 1) * 128], in_=hn[i][:, c * 128:(c + 1) * 128],
                                                      identity=self.ident[:]), reads=[bh[i], self.b_ident], writes=[pb])
                kb.op("dve", lambda e: e.tensor_tensor(
                    out=self.hT[:, :, t * 128:(t + 1) * 128], in0=psb[:, :].rearrange("p (c t) -> p c t", c=8),
                    in1=self.gmix[:, :].unsqueeze(2).to_broadcast([128, 8, 128]), op=ALU.mult),
                    reads=[pb, self.b_gmix], writes=[self.b_hT[t // 4]])

            for t in range(3):
                kb.dma(xt[t][:], self.x[t * 128:(t + 1) * 128, :], writes=[bx[t]])
            for t in range(NT):
                i = t % 2
                i4 = t % 4
                if t + 3 < NT:
                    kb.dma(xt[(t + 3) % 4][:], self.x[(t + 3) * 128:(t + 4) * 128, :], writes=[bx[(t + 3) % 4]])
                self.rms(xt[i4][:], bx[i4], junk, bj, ss[i], bs[i])
                kb.op("dve", lambda e: e.tensor_scalar(out=hn[i][:], in0=xt[i4][:], scalar1=ss[i][:, 0:1], scalar2=None,
                                                       op0=ALU.mult), reads=[bx[i4], bs[i]], writes=[bh[i]])
                if t > 0:
                    tail(t - 1)
            tail(NT - 1)
            kb.barrier()

    def phase2(self):
        kb = self.kb
        hT, bhT = self.hT, self.b_hT
        with contextlib.ExitStack() as es:
            self.make_wl(es, 3, 8, 128, "p2w")
            wq = [kb.sbuf_in(es, "p2wq%d" % i, [128, 8, 128], BF16) for i in range(2)]
            wk = [kb.sbuf_in(es, "p2wk%d" % i, [128, 8, 128], BF16) for i in range(2)]
            wv = [kb.sbuf_in(es, "p2wv%d" % i, [128, 8, 128], BF16) for i in range(2)]
            bwq = [Buf() for _ in range(2)]
            bwk = [Buf() for _ in range(2)]
            bwv = [Buf() for _ in range(2)]
            V = kb.sbuf_in(es, "p2V", [128, 3, 32, 2, 65], BF16)
            bV = [Buf() for _ in range(3)]
            qT = kb.sbuf_in(es, "p2qT", [128, S], BF16)
            kT = kb.sbuf_in(es, "p2kT", [128, S], BF16)
            bq, bk = Buf(), Buf()
            acc = kb.sbuf_in(es, "p2acc", [65, 2, S], F32)
            bacc = [Buf(), Buf()]
            bias = kb.sbuf_in(es, "p2bias", [128, 3, 2, 256], F32)
            bbias = Buf()
            NSC = 3
            sc = [kb.sbuf_in(es, "p2sc%d" % i, [128, 2, 512], F32) for i in range(NSC)]
            bsc = [Buf() for _ in range(NSC)]
            pT = [kb.sbuf_in(es, "p2pT%d" % i, [128, 2, 512], BF16) for i in range(NSC)]
            bpT = [Buf() for _ in range(NSC)]
            yst = [kb.sbuf_in(es, "p2y%d" % i, [64, S], BF16) for i in range(2)]
            byst = [Buf(), Buf()]
            isc = 0
            wi = 0
            kb.op("pool", lambda e: e.memset(V[:], 1.0), writes=bV)
            units = [(pp, gg) for pp in range(4) for gg in range(3)]

            def issue(pp, gg):
                return [self.load_w_dma(self.win(off + gg * 512 + pp * 128, 128), 8, 128) for off in (OFF_VA, OFF_QA, OFF_KA)]
            pend = [issue(*units.pop(0))]
            for p in range(4):
                kb.dma(bias[:], self.biasT.rearrange("k (g h c) -> k g h c", g=3, h=8)[:, :, 2 * p:2 * p + 2, :],
                       writes=[bbias])
                for g, d in enumerate(DILS):
                    kb.tick()
                    nb = 32 // d
                    w = wi % 2
                    wi += 1
                    toks = pend.pop(0)
                    self.load_w_cast(toks[0], wv[w][:], bwv[w])
                    self.load_w_cast(toks[1], wq[w][:], bwq[w])
                    self.load_w_cast(toks[2], wk[w][:], bwk[w])
                    nxt = units.pop(0) if units else None
                    if nxt is not None:
                        pend.append(issue(*nxt))
                    for tt in range(8):
                        ps, pb = self.nps()
                        for c in range(8):
                            kb.op("pe", lambda e: e.matmul(ps[:, :], lhsT=wv[w][:, c, :], rhs=hT[:, c, tt * 512:(tt + 1) * 512],
                                                           start=(c == 0), stop=(c == 7)), reads=[bhT[tt], bwv[w]], writes=[pb])
                        eng = "act" if tt % 2 else "dve"
                        kb.op(eng, lambda e: (e.copy if eng == "act" else e.tensor_copy)(
                            out=kT[:, tt * 512:(tt + 1) * 512], in_=ps[:, :]), reads=[pb], writes=[bk])
                    for b8 in range(0, 32, 8):
                        ps, pb = self.nps()
                        psb = ps.bitcast(BF16)
                        for bi in range(8):
                            r, m = divmod(b8 + bi, nb)
                            kb.op("pe", lambda e: e.transpose(out=psb[:, bi * 128:(bi + 1) * 128], in_=kT[:, _tok(d, r, m)],
                                                              identity=self.ident[:]), reads=[bk, self.b_ident], writes=[pb])
                        eng = "act" if (b8 // 8) % 2 else "dve"
                        kb.op(eng, lambda e: (e.copy if eng == "act" else e.tensor_copy)(
                            out=V[:, g, b8:b8 + 8, :, 0:64], in_=psb[:, :].rearrange("p (b h e) -> p b h e", b=8, h=2)),
                            reads=[pb], writes=[bV[g]])
                    for tt in range(8):
                        for (wt, bw, dst, bd, eng) in ((wq[w], bwq[w], qT, bq, "act"), (wk[w], bwk[w], kT, bk, "dve")):
                            ps, pb = self.nps()
                            for c in range(8):
                                kb.op("pe", lambda e: e.matmul(ps[:, :], lhsT=wt[:, c, :], rhs=hT[:, c, tt * 512:(tt + 1) * 512],
                                                               start=(c == 0), stop=(c == 7)), reads=[bhT[tt], bw], writes=[pb])
                            kb.op(eng, lambda e: (e.copy if eng == "act" else e.tensor_copy)(
                                out=dst[:, tt * 512:(tt + 1) * 512], in_=ps[:, :]), reads=[pb], writes=[bd])
                    its = [(r, m2) for r in range(d) for m2 in range(0, nb, 2)]
                    SK = 2
                    state = {}

                    def front(it):
                        nonlocal isc
                        r, m2 = it
                        j = isc % NSC
                        isc += 1
                        pss = [self.nps() for _ in range(2)]
                        for bi in range(2):
                            m = m2 + bi
                            tq = _tok(d, r, m)
                            tp = _tok(d, r, m - 1) if m > 0 else tq
                            for part, tk in ((0, tp), (1, tq)):
                                for hh in range(2):
                                    pr = slice(hh * 64, hh * 64 + 64)
                                    psS, pbS = pss[hh]
                                    c0 = bi * 256 + part * 128
                                    kb.op("pe", lambda e: e.matmul(psS[:, c0:c0 + 128], lhsT=kT[pr, tk], rhs=qT[pr, tq],
                                                                   start=True, stop=True), reads=[bq, bk], writes=[pbS])
                        for hh in range(2):
                            psS, pbS = pss[hh]
                            kb.op("dve", lambda e: e.scalar_tensor_tensor(
                                out=sc[j][:, hh, :].rearrange("p (b c) -> p b c", b=2), in0=psS[:, :].rearrange("p (b c) -> p b c", b=2),
                                scalar=0.125, in1=bias[:, g, hh, :].unsqueeze(1).to_broadcast([128, 2, 256]),
                                op0=ALU.mult, op1=ALU.add), reads=[pbS, bbias], writes=[bsc[j]])
                        kb.op("act", lambda e: e.activation(out=pT[j][:], in_=sc[j][:], func=AF.Exp),
                              reads=[bsc[j]], writes=[bpT[j]])
                        state[it] = j

                    def back(it):
                        r, m2 = it
                        j = state.pop(it)
                        for hh in range(2):
                            psO, pbO = self.nps()
                            for bi in range(2):
                                m = m2 + bi
                                if m > 0:
                                    kb.op("pe", lambda e: e.matmul(psO[0:65, bi * 128:(bi + 1) * 128], lhsT=V[:, g, r * nb + m - 1, hh, :],
                                                                   rhs=pT[j][:, hh, bi * 256:bi * 256 + 128], start=True, stop=False),
                                          reads=[bV[g], bpT[j]], writes=[pbO])
                                kb.op("pe", lambda e: e.matmul(psO[0:65, bi * 128:(bi + 1) * 128], lhsT=V[:, g, r * nb + m, hh, :],
                                                               rhs=pT[j][:, hh, bi * 256 + 128:bi * 256 + 256], start=(m == 0), stop=True),
                                      reads=[bV[g], bpT[j]], writes=[pbO])
                            dst = acc[:, hh, _tok(d, r, m2, 2)]
                            if g == 0:
                                kb.op("act", lambda e: e.copy(out=dst, in_=psO[0:65, 0:256]), reads=[pbO], writes=[bacc[hh]])
                            else:
                                kb.op("dve", lambda e: e.tensor_tensor(out=dst, in0=dst, in1=psO[0:65, 0:256], op=ALU.add),
                                      reads=[pbO, bacc[hh]], writes=[bacc[hh]])

                    for i in range(len(its) + SK):
                        if i < len(its):
                            front(its[i])
                        if i >= SK:
                            back(its[i - SK])
                for hh in range(2):
                    kb.op("act", lambda e: e.activation(out=acc[64:65, hh, :], in_=acc[64:65, hh, :], func=AF.Ln),
                          reads=[bacc[hh]], writes=[bacc[hh]])
                    kb.op("act", lambda e: e.activation(out=acc[64:65, hh, :], in_=acc[64:65, hh, :], func=AF.Exp, scale=-1.0),
                          reads=[bacc[hh]], writes=[bacc[hh]])
                    for tt in range(8):
                        sl = slice(tt * 512, (tt + 1) * 512)
                        ps, pb = self.nps()
                        kb.op("pe", lambda e: e.matmul(ps[0:64, :], lhsT=self.ones[64:65, 0:64], rhs=acc[64:65, hh, sl],
                                                       start=True, stop=True), reads=[self.b_ones, bacc[hh]], writes=[pb])
                        kb.op("dve", lambda e: e.tensor_tensor(out=yst[hh][:, sl], in0=acc[0:64, hh, sl], in1=ps[0:64, :],
                                                               op=ALU.mult), reads=[pb, bacc[hh]], writes=[byst[hh]])
                    hd = 2 * p + hh
                    kb.dma_later(2, self.ybT_d[hd * 64:(hd + 1) * 64, :], yst[hh][:], reads=[byst[hh]])
            kb.barrier()

    def phase3a(self):
        kb = self.kb
        hT, bhT = self.hT, self.b_hT
        es3 = contextlib.ExitStack()
        self.es3 = es3
        nm = ["ig", "lf", "bt", "blast", "bias_s", "w_inter", "wk", "decay", "e_s"]
        self.gt = {n: kb.sbuf_in(es3, "g_" + n, [128, 128], F32) for n in nm}
        self.bgt = {n: Buf() for n in nm}
        gt, bgt = self.gt, self.bgt
        def gates(es):
                wif = kb.sbuf_in(es, "p3wif", [128, 8, 8], BF16)
                bwif = Buf()
                wif_st = kb.sbuf_in(es, "p3wifst", [128, 8, 8], F32)
                bwif_st = Buf()
                kb.dma(wif_st[:], self.win(OFF_IF, 8), writes=[bwif_st])
                kb.op("dve", lambda e: e.tensor_copy(out=wif[:], in_=wif_st[:]), reads=[bwif_st], writes=[bwif])
                bif = kb.sbuf_in(es, "p3bif", [128, 8], F32)
                bbif = Buf()
                kb.dma(bif[:], self.b_if[:, :], writes=[bbif])
                gsb = kb.sbuf_in(es, "p3gsb", [128, 32, 8], F32)
                bgsb = Buf()
                tmp = kb.sbuf_in(es, "p3tmp", [128, 128], F32)
                btmp = Buf()
                ps, pb = self.nps()
                for t in range(NT):
                    for c in range(8):
                        kb.op("pe", lambda e: e.matmul(ps[:, t * 8:(t + 1) * 8], lhsT=hT[:, c, t * 128:(t + 1) * 128], rhs=wif[:, c, :],
                                                       start=(c == 0), stop=(c == 7)), reads=[bhT[t // 4], bwif], writes=[pb])
                kb.op("dve", lambda e: e.tensor_tensor(out=gsb[:], in0=ps[:, 0:256].rearrange("p (t g) -> p t g", g=8),
                                                       in1=bif[:, :].unsqueeze(1).to_broadcast([128, 32, 8]), op=ALU.add),
                      reads=[pb, bbif], writes=[bgsb])
                v3 = lambda a: a[:, :].rearrange("p (t h) -> p t h", h=4)
                kb.op("dve", lambda e: e.tensor_copy(out=v3(gt["ig"]), in_=gsb[:, :, 0:4]), reads=[bgsb], writes=[bgt["ig"]])
                kb.op("act", lambda e: e.activation(out=v3(tmp), in_=gsb[:, :, 4:8], func=AF.Exp, scale=-1.0),
                      reads=[bgsb], writes=[btmp])
                kb.op("act", lambda e: e.activation(out=tmp[:], in_=tmp[:], func=AF.Ln, bias=1.0), reads=[btmp], writes=[btmp])
                kb.op("dve", lambda e: e.tensor_scalar(out=gt["lf"][:], in0=tmp[:], scalar1=-1.0, scalar2=None, op0=ALU.mult),
                      reads=[btmp], writes=[bgt["lf"]])
                ps, pb = self.nps()
                kb.op("pe", lambda e: e.matmul(ps[:, 0:128], lhsT=self.tri[:], rhs=gt["lf"][:], start=True, stop=True),
                      reads=[self.b_tri, bgt["lf"]], writes=[pb])
                kb.op("dve", lambda e: e.tensor_copy(out=gt["bt"][:], in_=ps[:, 0:128]), reads=[pb], writes=[bgt["bt"]])
                ps, pb = self.nps()
                kb.op("pe", lambda e: e.matmul(ps[:, 0:128], lhsT=self.ones[:], rhs=gt["lf"][:], start=True, stop=True),
                      reads=[self.b_ones, bgt["lf"]], writes=[pb])
                kb.op("dve", lambda e: e.tensor_copy(out=gt["blast"][:], in_=ps[:, 0:128]), reads=[pb], writes=[bgt["blast"]])
                kb.op("dve", lambda e: e.scalar_tensor_tensor(out=gt["bias_s"][:], in0=gt["ig"][:], scalar=-LN16, in1=gt["bt"][:],
                                                              op0=ALU.add, op1=ALU.subtract),
                      reads=[bgt["ig"], bgt["bt"]], writes=[bgt["bias_s"]])
                kb.op("dve", lambda e: e.tensor_tensor(out=tmp[:], in0=gt["ig"][:], in1=gt["bt"][:], op=ALU.subtract),
                      reads=[bgt["ig"], bgt["bt"]], writes=[btmp])
                kb.op("act", lambda e: e.activation(out=gt["e_s"][:], in_=tmp[:], func=AF.Exp), reads=[btmp], writes=[bgt["e_s"]])
                kb.op("act", lambda e: e.activation(out=gt["w_inter"][:], in_=gt["bt"][:], func=AF.Exp, bias=-LN16),
                      reads=[bgt["bt"]], writes=[bgt["w_inter"]])
                kb.op("dve", lambda e: e.scalar_tensor_tensor(out=tmp[:], in0=gt["bias_s"][:], scalar=LN16, in1=gt["blast"][:],
                                                              op0=ALU.add, op1=ALU.add),
                      reads=[bgt["bias_s"], bgt["blast"]], writes=[btmp])
                kb.op("act", lambda e: e.activation(out=gt["wk"][:], in_=tmp[:], func=AF.Exp), reads=[btmp], writes=[bgt["wk"]])
                kb.op("act", lambda e: e.activation(out=gt["decay"][:], in_=gt["blast"][:], func=AF.Exp),
                      reads=[bgt["blast"]], writes=[bgt["decay"]])


        with contextlib.ExitStack() as es:
            self.make_wl(es, 2, 8, 128, "p3wm")
            cw = kb.sbuf_in(es, "p3cw", [128, 16, 4], F32)
            cb = kb.sbuf_in(es, "p3cb", [128, 16], F32)
            bcw = Buf()
            kb.dma(cw[:], self.conv_w[:, :, :], writes=[bcw])
            kb.dma(cb[:], self.conv_b[:, :], writes=[bcw])
            wqk = [kb.sbuf_in(es, "p3wqk%d" % i, [128, 8, 128], BF16) for i in range(2)]
            bwqk = [Buf(), Buf()]
            pre = [kb.sbuf_in(es, "p3pre%d" % i, [128, 3 + S], F32) for i in range(2)]
            bpre = [Buf(), Buf()]
            cacc = [kb.sbuf_in(es, "p3acc%d" % i, [128, S], F32) for i in range(2)]
            bcacc = [Buf(), Buf()]
            qo = [kb.sbuf_in(es, "p3qo%d" % i, [128, S], BF16) for i in range(1)]
            bqo = [Buf()]
            wvo = kb.sbuf_in(es, "p3wvo", [128, 8, 2048], BF16)
            bwvo = [Buf() for _ in range(4)]
            vst = [kb.sbuf_in(es, "p3vst%d" % i, [128, 1024], BF16) for i in range(2)]
            bvst = [Buf(), Buf()]
            ost = [kb.sbuf_in(es, "p3ost%d" % i, [128, 1024], BF16) for i in range(2)]
            bost = [Buf(), Buf()]
            for i in range(2):
                kb.op("pool", lambda e: e.memset(pre[i][:, 0:3], 0.0), writes=[bpre[i]])

            def qk_front(j):
                i = j % 2
                self.load_w(self.win(j * 128, 128), wqk[i][:], bwqk[i], 8, 128, eng="act")
                for tt in range(8):
                    ps, pb = self.nps()
                    for c in range(8):
                        kb.op("pe", lambda e: e.matmul(ps[:, :], lhsT=wqk[i][:, c, :], rhs=hT[:, c, tt * 512:(tt + 1) * 512],
                                                       start=(c == 0), stop=(c == 7)), reads=[bhT[tt], bwqk[i]], writes=[pb])
                    kb.op("act", lambda e: e.copy(out=pre[i][:, 3 + tt * 512:3 + (tt + 1) * 512], in_=ps[:, :]),
                          reads=[pb], writes=[bpre[i]])

            def qk_conv(j):
                i = j % 2
                kb.op("dve", lambda e: e.tensor_scalar(out=cacc[i][:], in0=pre[i][:, 0:S], scalar1=cw[:, j, 0:1], scalar2=None,
                                                       op0=ALU.mult), reads=[bpre[i], bcw], writes=[bcacc[i]])
                for tap in range(1, 4):
                    kb.op("dve", lambda e: e.scalar_tensor_tensor(out=cacc[i][:], in0=pre[i][:, tap:tap + S], scalar=cw[:, j, tap:tap + 1],
                                                                  in1=cacc[i][:], op0=ALU.mult, op1=ALU.add),
                          reads=[bpre[i], bcw, bcacc[i]], writes=[bcacc[i]])

            def qk_back(j):
                i = j % 2
                kb.op("act", lambda e: e.activation(out=qo[0][:], in_=cacc[i][:], func=AF.Silu, bias=cb[:, j:j + 1]),
                      reads=[bcacc[i], bcw], writes=[bqo[0]])
                dstd = self.qT_d if j < 8 else self.kT_d
                jj = j % 8
                kb.dma(dstd[jj * 128:(jj + 1) * 128, :], qo[0][:], reads=[bqo[0]], q="act")

            def vo_tile(t):
                i = t % 2
                for q4 in range(4):
                    ps, pb = self.nps()
                    for c in range(8):
                        kb.op("pe", lambda e: e.matmul(ps[:, :], lhsT=hT[:, c, t * 128:(t + 1) * 128], rhs=wvo[:, c, q4 * 512:(q4 + 1) * 512],
                                                       start=(c == 0), stop=(c == 7)), reads=[bhT[t // 4], bwvo[q4]], writes=[pb])
                    hs = slice((q4 % 2) * 512, (q4 % 2) * 512 + 512)
                    if q4 < 2:
                        kb.op("act", lambda e: e.copy(out=vst[i][:, hs], in_=ps[:, :]), reads=[pb], writes=[bvst[i]])
                    else:
                        kb.op("act", lambda e: e.activation(out=ost[i][:, hs], in_=ps[:, :], func=AF.Sigmoid), reads=[pb], writes=[bost[i]])
                kb.dma(self.v_d[t * 128:(t + 1) * 128, :], vst[i][:], reads=[bvst[i]], q="act")
                kb.dma(self.og_d[t * 128:(t + 1) * 128, :], ost[i][:], reads=[bost[i]], q="act")

            qk_front(0)
            for q4 in range(4):
                for k4 in range(4):
                    c0 = (OFF_VM if q4 < 2 else OFF_OM) + (q4 % 2) * 512 + k4 * 128
                    self.load_w(self.win(c0, 128), wvo[:, :, q4 * 512 + k4 * 128:q4 * 512 + (k4 + 1) * 128], bwvo[q4], 8, 128, eng="act")
            vt = 0
            for j in range(16):
                qk_conv(j)
                if j + 1 < 16:
                    qk_front(j + 1)
                if j >= 2:
                    vo_tile(vt)
                    vt += 1
                if j >= 1:
                    qk_back(j - 1)
                if j == 1:
                    gates(es)
                if j >= 2:
                    vo_tile(vt)
                    vt += 1
            qk_back(15)
            while vt < NT:
                vo_tile(vt)
                vt += 1
        kb.barrier()

    def phase3b(self):
        kb = self.kb
        gt, bgt = self.gt, self.bgt
        with contextlib.ExitStack() as es:
            qw = [kb.sbuf_in(es, "r_q%d" % i, [128, 8, 512], BF16) for i in range(2)]
            kw_ = [kb.sbuf_in(es, "r_k%d" % i, [128, 8, 512], BF16) for i in range(2)]
            vw = [kb.sbuf_in(es, "r_v%d" % i, [128, 4, 4, 260], BF16) for i in range(2)]
            ow = [kb.sbuf_in(es, "r_o%d" % i, [128, 4, 1024], BF16) for i in range(2)]
            yaw = [kb.sbuf_in(es, "r_y%d" % i, [128, 8, 512], BF16) for i in range(2)]
            bq = [Buf(), Buf()]
            bk = [Buf(), Buf()]
            bv = [Buf(), Buf()]
            bo = [Buf(), Buf()]
            by = [Buf(), Buf()]
            C32 = kb.sbuf_in(es, "r_C32", [128, 4, 2, 257], F32)
            Cb = kb.sbuf_in(es, "r_Cb", [128, 2, 4, 2, 260], BF16)
            bC32 = [Buf() for _ in range(4)]
            bCb = [[Buf() for _ in range(4)] for _ in range(2)]
            kb.op("pool", lambda e: e.memset(C32[:], 0.0), writes=bC32)
            kb.op("pool", lambda e: e.memset(Cb[:, :, :, :, :].rearrange("p a h d e -> p (a h d e)"), 0.0), writes=bCb[0] + bCb[1])
            for i in range(2):
                kb.op("pool", lambda e: e.memset(vw[i][:], 1.0), writes=[bv[i]])

            def rot(name, shape, dt, n):
                return [kb.sbuf_in(es, "%s%d" % (name, i), shape, dt) for i in range(n)], [Buf() for _ in range(n)]
            WT, bWT = rot("r_WT", [128, 4, 128], BF16, 2)
            kws, bkws = rot("r_kws", [128, 4, 256], BF16, 2)
            num, bnum = rot("r_num", [128, 4, 257], F32, 3)
            yn, byn = rot("r_yn", [128, 4, 256], F32, 2)
            y2, by2 = rot("r_y2", [128, 1024], BF16, 2)
            st6, bst6 = rot("r_st6", [128, 4, 6], F32, 2)
            mv, bmv = rot("r_mv", [128, 4, 2], F32, 2)
            sm, bsm = rot("r_sm", [128, 3, 4], F32, 2)

            def load_window(w):
                wi = w % 2
                tsl = slice(w * 512, (w + 1) * 512)
                kb.dma(qw[wi][:], self.qT_d.rearrange("(c p) t -> p c t", p=128)[:, :, tsl], writes=[bq[wi]])
                kb.dma(kw_[wi][:], self.kT_d.rearrange("(c p) t -> p c t", p=128)[:, :, tsl], writes=[bk[wi]])
                for c4 in range(4):
                    t0 = w * 512 + c4 * 128
                    kb.dma(vw[wi][:, c4, :, 0:256], self.v_d[t0:t0 + 128, :].rearrange("p (h e) -> p h e", h=4), writes=[bv[wi]])
                kb.dma(ow[wi][:], self.og_d[w * 512:(w + 1) * 512, :].rearrange("(c p) n -> p c n", p=128), writes=[bo[wi]])

            def ln_a(c, w, wi, cc, csl, n2, n3):
                nm = num[n3]
                kb.op("dve", lambda e: e.tensor_scalar(out=sm[n2][:, 0, :], in0=nm[:, :, 256], scalar1=-1.0, scalar2=1.0,
                                                       op0=ALU.mult, op1=ALU.max), reads=[bnum[n3]], writes=[bsm[n2]])
                kb.op("dve", lambda e: e.tensor_tensor(out=sm[n2][:, 0, :], in0=sm[n2][:, 0, :], in1=nm[:, :, 256], op=ALU.max),
                      reads=[bnum[n3], bsm[n2]], writes=[bsm[n2]])
                for h in range(4):
                    kb.op("dve", lambda e: e.bn_stats(out=st6[n2][:, h, :], in_=nm[:, h, 0:256]), reads=[bnum[n3]], writes=[bst6[n2]])
                for h in range(4):
                    kb.op("dve", lambda e: e.bn_aggr(out=mv[n2][:, h, :], in_=st6[n2][:, h, :]), reads=[bst6[n2]], writes=[bmv[n2]])
                kb.op("dve", lambda e: e.tensor_tensor(out=sm[n2][:, 1, :], in0=sm[n2][:, 0, :], in1=sm[n2][:, 0, :], op=ALU.mult),
                      reads=[bsm[n2]], writes=[bsm[n2]])
                kb.op("dve", lambda e: e.scalar_tensor_tensor(out=sm[n2][:, 1, :], in0=sm[n2][:, 1, :], scalar=EPS, in1=mv[n2][:, :, 1],
                                                              op0=ALU.mult, op1=ALU.add), reads=[bsm[n2], bmv[n2]], writes=[bsm[n2]])
                kb.op("act", lambda e: e.activation(out=sm[n2][:, 2, :], in_=sm[n2][:, 1, :], func=AF.Sqrt), reads=[bsm[n2]], writes=[bsm[n2]])

            def ln_b(c, w, wi, cc, csl, n2, n3):
                nm = num[n3]
                kb.op("dve", lambda e: e.reciprocal(out=sm[n2][:, 2, :], in_=sm[n2][:, 2, :]), reads=[bsm[n2]], writes=[bsm[n2]])
                for h in range(4):
                    kb.op("dve", lambda e: e.tensor_scalar(out=yn[n2][:, h, :], in0=nm[:, h, 0:256], scalar1=mv[n2][:, h, 0:1],
                                                           scalar2=sm[n2][:, 2, h:h + 1], op0=ALU.subtract, op1=ALU.mult),
                          reads=[bnum[n3], bmv[n2], bsm[n2]], writes=[byn[n2]])
                kb.op("pool", lambda e: e.tensor_tensor(out=y2[n2][:], in0=yn[n2][:, :, :].rearrange("p h e -> p (h e)"),
                                                        in1=ow[wi][:, cc, :], op=ALU.mult), reads=[byn[n2], bo[wi]], writes=[by2[n2]])

            def ln_c(c, w, wi, cc, csl, n2, n3):
                psY, pbY = self.nps()
                psYb = psY.bitcast(BF16)
                for j in range(8):
                    kb.op("pe", lambda e: e.transpose(out=psYb[:, j * 128:(j + 1) * 128], in_=y2[n2][:, j * 128:(j + 1) * 128],
                                                      identity=self.ident[:]), reads=[by2[n2], self.b_ident], writes=[pbY])
                kb.op("act", lambda e: e.copy(out=yaw[wi][:, :, csl], in_=psYb[:, :].rearrange("p (j t) -> p j t", j=8)),
                      reads=[pbY], writes=[by[wi]])
                if cc == 3:
                    kb.dma(self.yaT_d.rearrange("(c p) t -> p c t", p=128)[:, :, w * 512:(w + 1) * 512], yaw[wi][:], reads=[by[wi]], q="act")

            def ln_tick(t):
                for fn, lag in ((ln_a, 1), (ln_b, 2), (ln_c, 3)):
                    cz = t - lag
                    if 0 <= cz < NT:
                        fn(*ln_args[cz])

            ln_args = {}
            load_window(0)
            stS = {}
            self.ps_lo = 2

            def stage1(c):
                w, cc = divmod(c, 4)
                wi = w % 2
                csl = slice(cc * 128, (cc + 1) * 128)
                psS, pbS = self.ps[0], self.pb[0]
                psK, pbK = self.ps[1], self.pb[1]
                psKb = psK.bitcast(BF16)
                for h in range(4):
                    for dc in range(2):
                        kb.op("pe", lambda e: e.matmul(psS[:, h * 128:(h + 1) * 128], lhsT=kw_[wi][:, 2 * h + dc, csl], rhs=qw[wi][:, 2 * h + dc, csl],
                                                       start=(dc == 0), stop=(dc == 1)), reads=[bk[wi], bq[wi]], writes=[pbS])
                for h in range(4):
                    for dc in range(2):
                        kb.op("pe", lambda e: e.transpose(out=psKb[:, (2 * h + dc) * 128:(2 * h + dc + 1) * 128], in_=kw_[wi][:, 2 * h + dc, csl],
                                                          identity=self.ident[:]), reads=[bk[wi], self.b_ident], writes=[pbK])
                stS[c] = (psS, pbS, psKb, pbK)

            stage1(0)
            for c in range(NT):
                w, cc = divmod(c, 4)
                wi = w % 2
                if cc == 2 and w + 1 < 8:
                    load_window(w + 1)
                csl = slice(cc * 128, (cc + 1) * 128)
                n2 = c % 2
                n3 = c % 3
                psS, pbS, psKb, pbK = stS.pop(c)
                for h in range(4):
                    ch = c * 4 + h
                    kb.op("dve", lambda e: e.scalar_tensor_tensor(out=WT[n2][:, h, :], in0=psS[:, h * 128:(h + 1) * 128],
                                                                  scalar=gt["e_s"][:, ch:ch + 1], in1=self.tri[:], op0=ALU.mult, op1=ALU.mult),
                          reads=[pbS, bgt["e_s"], self.b_tri], writes=[bWT[n2]])
                for h in range(4):
                    ch = c * 4 + h
                    kb.op("act", lambda e: e.activation(out=kws[n2][:, h, :], in_=psKb[:, h * 256:(h + 1) * 256], func=AF.Copy,
                                                        scale=gt["wk"][:, ch:ch + 1]), reads=[pbK, bgt["wk"]], writes=[bkws[n2]])
                for h in range(4):
                    ch = c * 4 + h
                    psN, pbN = self.nps()
                    kb.op("pe", lambda e: e.matmul(psN[:, 0:257], lhsT=WT[n2][:, h, :], rhs=vw[wi][:, cc, h, 0:257], start=True, stop=False),
                          reads=[bWT[n2], bv[wi]], writes=[pbN])
                    for dc in range(2):
                        kb.op("pe", lambda e: e.matmul(psN[:, 0:257], lhsT=qw[wi][:, 2 * h + dc, csl], rhs=Cb[:, c % 2, h, dc, 0:257],
                                                       start=False, stop=(dc == 1)), reads=[bq[wi], bCb[c % 2][h]], writes=[pbN])
                    kb.op("act", lambda e: e.activation(out=num[n3][:, h, :], in_=psN[:, 0:257], func=AF.Copy,
                                                        scale=gt["w_inter"][:, ch:ch + 1]), reads=[pbN, bgt["w_inter"]], writes=[bnum[n3]])
                ln_args[c] = (c, w, wi, cc, csl, n2, n3)
                ln_tick(c)
                if c < NT - 1:
                    for h in range(4):
                        ch = c * 4 + h
                        for dc in range(2):
                            psC, pbC = self.nps()
                            kb.op("pe", lambda e: e.matmul(psC[:, 0:257], lhsT=kws[n2][:, h, dc * 128:(dc + 1) * 128], rhs=vw[wi][:, cc, h, 0:257],
                                                           start=True, stop=True), reads=[bkws[n2], bv[wi]], writes=[pbC])
                            kb.op("dve", lambda e: e.scalar_tensor_tensor(out=C32[:, h, dc, :], in0=C32[:, h, dc, :],
                                                                          scalar=gt["decay"][:, ch:ch + 1], in1=psC[:, 0:257],
                                                                          op0=ALU.mult, op1=ALU.add),
                                  reads=[pbC, bgt["decay"], bC32[h]], writes=[bC32[h]])
                        kb.op("act", lambda e: e.copy(out=Cb[:, (c + 1) % 2, h, :, 0:257], in_=C32[:, h, :, :]), reads=[bC32[h]],
                              writes=[bCb[(c + 1) % 2][h]])
                if c + 1 < NT:
                    stage1(c + 1)
            for t in range(NT, NT + 3):
                ln_tick(t)
        self.ps_lo = 0
        self.es3.close()
        self.es3 = None
        kb.barrier()

    def phase4(self):
        kb = self.kb
        hT, bhT = self.hT, self.b_hT
        with contextlib.ExitStack() as es:
            self.make_wl(es, 2, 8, 256, "p4w")
            Wa = kb.sbuf_in(es, "p4Wa", [128, 8, 1024], BF16)
            Wb = kb.sbuf_in(es, "p4Wb", [128, 4, 1024], BF16)
            Wg = kb.sbuf_in(es, "p4Wg", [128, 8, 2048], BF16)
            ya = [kb.sbuf_in(es, "p4ya%d" % i, [128, 8, 512], BF16) for i in range(2)]
            yb = [kb.sbuf_in(es, "p4yb%d" % i, [128, 4, 512], BF16) for i in range(2)]
            bya, byb = [Buf(), Buf()], [Buf(), Buf()]
            for T0 in range(2):
                kb.dma(ya[T0][:], self.yaT_d.rearrange("(c p) t -> p c t", p=128)[:, :, T0 * 512:(T0 + 1) * 512], writes=[bya[T0]])
                kb.dma(yb[T0][:], self.ybT_d.rearrange("(c p) t -> p c t", p=128)[:, :, T0 * 512:(T0 + 1) * 512], writes=[byb[T0]])
            bWa = [Buf() for _ in range(4)]
            bWb = [Buf() for _ in range(4)]
            bWg = [Buf() for _ in range(8)]
            def load_set(q):
                cs = slice(q * 256, (q + 1) * 256)
                self.load_w(self.w_a[:, cs].rearrange("(c p) n -> p c n", p=128), Wa[:, :, cs], bWa[q], 8, 256, self.gml, self.b_gml)
                self.load_w(self.w_b[:, cs].rearrange("(c p) n -> p c n", p=128), Wb[:, :, cs], bWb[q], 4, 256)
                for q2 in (q, q + 4):
                    self.load_w(self.win(OFF_GATE + q2 * 256, 256), Wg[:, :, q2 * 256:(q2 + 1) * 256], bWg[q2], 8, 256)
            mT = [kb.sbuf_in(es, "p4mT%d" % i, [128, 8, 512], BF16) for i in range(2)]
            bmT = [Buf(), Buf()]
            sa = [kb.sbuf_in(es, "p4sa%d" % i, [128, 512], F32) for i in range(2)]
            sb_ = [kb.sbuf_in(es, "p4sb%d" % i, [128, 512], F32) for i in range(2)]
            bsa, bsb = [Buf(), Buf()], [Buf(), Buf()]
            k2 = 0

            def ld_y(T):
                ti = T % 2
                tsl = slice(T * 512, (T + 1) * 512)
                kb.dma(ya[ti][:], self.yaT_d.rearrange("(c p) t -> p c t", p=128)[:, :, tsl], writes=[bya[ti]])
                kb.dma(yb[ti][:], self.ybT_d.rearrange("(c p) t -> p c t", p=128)[:, :, tsl], writes=[byb[ti]])
            def unit(T, fc):
                nonlocal k2
                ti = T % 2
                tsl = slice(T * 512, (T + 1) * 512)
                fs = slice(fc * 128, (fc + 1) * 128)
                psA, pbA = self.nps()
                for c in range(8):
                    kb.op("pe", lambda e: e.matmul(psA[:, :], lhsT=Wa[:, c, fs], rhs=ya[ti][:, c, :], start=(c == 0), stop=(c == 7)),
                          reads=[bWa[fc // 2], bya[ti]], writes=[pbA])
                psB, pbB = self.nps()
                for c in range(4):
                    kb.op("pe", lambda e: e.matmul(psB[:, :], lhsT=Wb[:, c, fs], rhs=yb[ti][:, c, :], start=(c == 0), stop=(c == 3)),
                          reads=[bWb[fc // 2], byb[ti]], writes=[pbB])
                psGA, pbGA = self.nps()
                for c in range(8):
                    kb.op("pe", lambda e: e.matmul(psGA[:, :], lhsT=Wg[:, c, fs], rhs=hT[:, c, tsl], start=(c == 0), stop=(c == 7)),
                          reads=[bWg[fc // 2], bhT[T]], writes=[pbGA])
                psGB, pbGB = self.nps()
                for c in range(8):
                    kb.op("pe", lambda e: e.matmul(psGB[:, :], lhsT=Wg[:, c, 1024 + fc * 128:1024 + (fc + 1) * 128], rhs=hT[:, c, tsl],
                                                   start=(c == 0), stop=(c == 7)), reads=[bWg[4 + fc // 2], bhT[T]], writes=[pbGB])
                i2 = k2 % 2
                k2 += 1
                kb.op("act", lambda e: e.activation(out=sa[i2][:], in_=psGA[:, :], func=AF.Sigmoid), reads=[pbGA], writes=[bsa[i2]])
                kb.op("act", lambda e: e.activation(out=sb_[i2][:], in_=psGB[:, :], func=AF.Sigmoid), reads=[pbGB], writes=[bsb[i2]])
                kb.op("dve", lambda e: e.tensor_tensor(out=sa[i2][:], in0=sa[i2][:], in1=psA[:, :], op=ALU.mult),
                      reads=[bsa[i2], pbA], writes=[bsa[i2]])
                kb.op("dve", lambda e: e.tensor_tensor(out=sb_[i2][:], in0=sb_[i2][:], in1=psB[:, :], op=ALU.mult),
                      reads=[bsb[i2], pbB], writes=[bsb[i2]])
                kb.op("dve", lambda e: e.tensor_tensor(out=mT[ti][:, fc, :], in0=sa[i2][:], in1=sb_[i2][:], op=ALU.add),
                      reads=[bsa[i2], bsb[i2]], writes=[bmT[ti]])

            def store(T):
                ti = T % 2
                kb.tick()
                kb.dma_later(1, self.mT_d.rearrange("(c p) t -> p c t", p=128)[:, :, T * 512:(T + 1) * 512], mT[ti][:], reads=[bmT[ti]])

            for fc in range(8):
                if fc % 2 == 0:
                    load_set(fc // 2)
                unit(0, fc)
                unit(1, fc)
            store(0)
            store(1)
            for T in range(2, 8):
                ld_y(T)
                for fc in range(8):
                    unit(T, fc)
                store(T)
            kb.barrier()

    def phase4b(self):
        kb = self.kb
        with contextlib.ExitStack() as es:
            self.make_wl(es, 2, 8, 256, "p4bw")
            Wo = kb.sbuf_in(es, "p4Wo", [128, 8, 1024], BF16)
            bWo = [Buf() for _ in range(4)]
            mT = [kb.sbuf_in(es, "p4bm%d" % i, [128, 8, 512], BF16) for i in range(2)]
            bmT = [Buf(), Buf()]
            hfs = [kb.sbuf_in(es, "p4hf%d" % i, [128, 8, 512], BF16) for i in range(2)]
            bhfs = [Buf(), Buf()]
            NB = 3
            xt = [kb.sbuf_in(es, "p4x%d" % i, [128, D], F32) for i in range(NB)]
            bx = [Buf() for _ in range(NB)]
            x1 = [kb.sbuf_in(es, "p4x1%d" % i, [128, D], F32) for i in range(NB)]
            bx1 = [Buf() for _ in range(NB)]
            junk = kb.sbuf_in(es, "p4j", [128, D], BF16)
            bj = Buf()
            ss = [kb.sbuf_in(es, "p4s%d" % i, [128, 1], F32) for i in range(NB)]
            bs = [Buf() for _ in range(NB)]
            hn = [kb.sbuf_in(es, "p4h%d" % i, [128, D], BF16) for i in range(NB)]
            bh = [Buf() for _ in range(NB)]

            def front(t):
                T, u = divmod(t, 4)
                ti = T % 2
                i = t % NB
                if u == 0 and t > 0:
                    kb.dma(mT[ti][:], self.mT_d.rearrange("(c p) t -> p c t", p=128)[:, :, T * 512:(T + 1) * 512], writes=[bmT[ti]])
                usl = slice(u * 128, (u + 1) * 128)
                if t > 0:
                    kb.dma(xt[i][:], self.x[t * 128:(t + 1) * 128, :], writes=[bx[i]])
                for half in range(2):
                    hs = slice(half * 512, (half + 1) * 512)
                    ps, pb = self.nps()
                    for c in range(8):
                        kb.op("pe", lambda e: e.matmul(ps[:, :], lhsT=mT[ti][:, c, usl], rhs=Wo[:, c, hs], start=(c == 0), stop=(c == 7)),
                              reads=[bmT[ti], bWo[2 * half], bWo[2 * half + 1]], writes=[pb])
                    kb.op("dve", lambda e: e.tensor_tensor(out=x1[i][:, hs], in0=xt[i][:, hs], in1=ps[:, :], op=ALU.add),
                          reads=[bx[i], pb], writes=[bx1[i]])
                kb.tick()
                kb.dma_later(1, self.x1_d[t * 128:(t + 1) * 128, :], x1[i][:], reads=[bx1[i]])
                kb.op("act", lambda e: e.activation(out=junk[:], in_=x1[i][:], func=AF.Square, scale=float(D ** -0.5),
                                                    accum_out=ss[i][:]), reads=[bx1[i]], writes=[bj, bs[i]])
                kb.op("act", lambda e: e.activation(out=ss[i][:], in_=ss[i][:], func=AF.Sqrt, bias=EPS), reads=[bs[i]], writes=[bs[i]])

            def front_b(t):
                i = t % NB
                kb.op("dve", lambda e: e.reciprocal(out=ss[i][:], in_=ss[i][:]), reads=[bs[i]], writes=[bs[i]])
                kb.op("dve", lambda e: e.tensor_scalar(out=hn[i][:], in0=x1[i][:], scalar1=ss[i][:, 0:1], scalar2=None,
                                                       op0=ALU.mult), reads=[bx1[i], bs[i]], writes=[bh[i]])

            def back(t):
                T, u = divmod(t, 4)
                ti = T % 2
                i = t % NB
                usl = slice(u * 128, (u + 1) * 128)
                ps, pb = self.nps()
                psb = ps.bitcast(BF16)
                for c in range(8):
                    kb.op("pe", lambda e: e.transpose(out=psb[:, c * 128:(c + 1) * 128], in_=hn[i][:, c * 128:(c + 1) * 128],
                                                      identity=self.ident[:]), reads=[bh[i], self.b_ident], writes=[pb])
                kb.op("dve", lambda e: e.tensor_tensor(out=hfs[ti][:, :, usl], in0=psb[:, :].rearrange("p (c t) -> p c t", c=8),
                                                       in1=self.gffn[:, :].unsqueeze(2).to_broadcast([128, 8, 128]), op=ALU.mult),
                      reads=[pb, self.b_gffn], writes=[bhfs[ti]])
                if u == 3:
                    kb.dma_later(2, self.hfT_d.rearrange("(c p) t -> p c t", p=128)[:, :, T * 512:(T + 1) * 512], hfs[ti][:], reads=[bhfs[ti]])

            kb.dma(mT[0][:], self.mT_d.rearrange("(c p) t -> p c t", p=128)[:, :, 0:512], writes=[bmT[0]])
            kb.dma(xt[0][:], self.x[0:128, :], writes=[bx[0]])
            for q in range(4):
                cs = slice(q * 256, (q + 1) * 256)
                self.load_w(self.w_o[:, cs].rearrange("(c p) n -> p c n", p=128), Wo[:, :, cs], bWo[q], 8, 256)
            front(0)
            front_b(0)
            for t in range(NT):
                if t + 1 < NT:
                    front(t + 1)
                back(t)
                if t + 1 < NT:
                    front_b(t + 1)
            kb.barrier()

    def phase5(self):
        kb = self.kb
        with contextlib.ExitStack() as es:
            self.make_wl(es, 3, 8, 256, "p5w")
            Wd = kb.sbuf_in(es, "p5Wd", [128, NFF, 1024], BF16)
            bWd = [Buf() for _ in range(NFF)]
            dst_ = [kb.sbuf_in(es, "p5ds%d" % i, [128, 1024], F32) for i in range(2)]
            bds = [Buf(), Buf()]
            def load_wd(f):
                i = f % 2
                kb.dma(dst_[i][:], self.w_down[f * 128:(f + 1) * 128, :], writes=[bds[i]])
                kb.op("act" if f % 2 else "dve", lambda e: (e.copy if f % 2 else e.tensor_copy)(out=Wd[:, f, :], in_=dst_[i][:]),
                      reads=[bds[i]], writes=[bWd[f]])
            gfin = kb.sbuf_in(es, "p5gf", [128, 1024], F32)
            bgfin = Buf()
            kb.dma(gfin[:], self.g_fin[:, :], writes=[bgfin])
            hf = [kb.sbuf_in(es, "p5hf%d" % i, [128, 8, 1024], BF16) for i in range(2)]
            bhf = [Buf(), Buf()]
            aT = kb.sbuf_in(es, "p5aT", [128, NFF, 1024], BF16)
            baT = [Buf() for _ in range(NFF)]
            wg = [kb.sbuf_in(es, "p5wg%d" % i, [128, 8, 256], BF16) for i in range(2)]
            wu = [kb.sbuf_in(es, "p5wu%d" % i, [128, 8, 256], BF16) for i in range(2)]
            bwg, bwu = [Buf(), Buf()], [Buf(), Buf()]
            sg = [kb.sbuf_in(es, "p5sg%d" % i, [128, 512], F32) for i in range(2)]
            bsg = [Buf(), Buf()]
            xt = [kb.sbuf_in(es, "p5x%d" % i, [128, D], F32) for i in range(2)]
            bx = [Buf(), Buf()]
            x2 = [kb.sbuf_in(es, "p5x2%d" % i, [128, D], F32) for i in range(2)]
            bx2 = [Buf(), Buf()]
            junk = kb.sbuf_in(es, "p5j", [128, D], BF16)
            bj = Buf()
            ss = [kb.sbuf_in(es, "p5s%d" % i, [128, 1], F32) for i in range(2)]
            bs = [Buf(), Buf()]
            ot = [kb.sbuf_in(es, "p5o%d" % i, [128, D], F32) for i in range(2)]
            bot = [Buf(), Buf()]
            k2 = 0
            wi = 0
            for ST in range(4):
                si = ST % 2
                ssl = slice(ST * 1024, (ST + 1) * 1024)
                if ST == 0:
                    kb.dma(hf[0][:], self.hfT_d.rearrange("(c p) t -> p c t", p=128)[:, :, 0:1024], writes=[bhf[0]])
                if ST + 1 < 4:
                    kb.dma(hf[1 - si][:], self.hfT_d.rearrange("(c p) t -> p c t", p=128)[:, :, (ST + 1) * 1024:(ST + 2) * 1024],
                           writes=[bhf[1 - si]])
                for f2 in range(NFF // 2):
                    w = wi % 2
                    wi += 1
                    cs = slice(f2 * 256, (f2 + 1) * 256)
                    self.load_w(self.w_gate[:, cs].rearrange("(c p) n -> p c n", p=128), wg[w][:], bwg[w], 8, 256)
                    self.load_w(self.w_up[:, cs].rearrange("(c p) n -> p c n", p=128), wu[w][:], bwu[w], 8, 256)
                    for fi in range(2):
                        f = f2 * 2 + fi
                        fs = slice(fi * 128, (fi + 1) * 128)
                        for tt in range(2):
                            ts_ = slice(tt * 512, (tt + 1) * 512)
                            psG, pbG = self.nps()
                            for c in range(8):
                                kb.op("pe", lambda e: e.matmul(psG[:, :], lhsT=wg[w][:, c, fs], rhs=hf[si][:, c, ts_], start=(c == 0), stop=(c == 7)),
                                      reads=[bwg[w], bhf[si]], writes=[pbG])
                            psU, pbU = self.nps()
                            for c in range(8):
                                kb.op("pe", lambda e: e.matmul(psU[:, :], lhsT=wu[w][:, c, fs], rhs=hf[si][:, c, ts_], start=(c == 0), stop=(c == 7)),
                                      reads=[bwu[w], bhf[si]], writes=[pbU])
                            i2 = k2 % 2
                            k2 += 1
                            kb.op("act", lambda e: e.activation(out=sg[i2][:], in_=psG[:, :], func=AF.Silu), reads=[pbG], writes=[bsg[i2]])
                            kb.op("dve", lambda e: e.tensor_tensor(out=aT[:, f, ts_], in0=sg[i2][:], in1=psU[:, :], op=ALU.mult),
                                  reads=[bsg[i2], pbU], writes=[baT[f]])
                    if ST == 0:
                        load_wd(2 * f2)
                        load_wd(2 * f2 + 1)
                kb.dma(xt[(ST * 8) % 2][:], self.x1_d[ST * 1024:ST * 1024 + 128, :], writes=[bx[(ST * 8) % 2]])
                for u in range(8):
                    t = ST * 8 + u
                    i = t % 2
                    usl = slice(u * 128, (u + 1) * 128)
                    if u + 1 < 8:
                        kb.dma(xt[1 - i][:], self.x1_d[(t + 1) * 128:(t + 2) * 128, :], writes=[bx[1 - i]])
                    for half in range(2):
                        hs = slice(half * 512, (half + 1) * 512)
                        ps, pb = self.nps()
                        for f in range(NFF):
                            kb.op("pe", lambda e: e.matmul(ps[:, :], lhsT=aT[:, f, usl], rhs=Wd[:, f, hs], start=(f == 0), stop=(f == NFF - 1)),
                                  reads=[baT[f], bWd[f]], writes=[pb])
                        kb.op("dve", lambda e: e.tensor_tensor(out=x2[i][:, hs], in0=xt[i][:, hs], in1=ps[:, :], op=ALU.add),
                              reads=[bx[i], pb], writes=[bx2[i]])
                    self.rms(x2[i][:], bx2[i], junk, bj, ss[i], bs[i])
                    kb.op("dve", lambda e: e.tensor_tensor(out=x2[i][:], in0=x2[i][:], in1=gfin[:], op=ALU.mult),
                          reads=[bx2[i], bgfin], writes=[bx2[i]])
                    kb.op("act", lambda e: e.activation(out=ot[i][:], in_=x2[i][:], func=AF.Copy, scale=ss[i][:, 0:1]),
                          reads=[bx2[i], bs[i]], writes=[bot[i]])
                    kb.dma(self.y[t * 128:(t + 1) * 128, :], ot[i][:], reads=[bot[i]], q="act")
            kb.barrier()


def _rel_bucket(n):
    n = np.asarray(n, dtype=np.int64)
    nf = np.maximum(n, 1).astype(np.float32)
    large = 16 + (np.log(nf / np.float32(16.0)) / np.float32(np.log(128.0)) * np.float32(16.0)).astype(np.int32)
    large = np.minimum(large, 31)
    return np.where(n < 16, n, large)


def _bias_tables(rel_bias):
    out = np.full((128, 3, 8, 2, 128), -30000.0, np.float32)
    ik = np.arange(128)[:, None]
    iq = np.arange(128)[None, :]
    for g, d in enumerate(DILS):
        for j in range(2):
            dist = (128 + iq) - (ik + 128 * j)
            valid = (dist >= 0) & (dist <= 128)
            bucket = _rel_bucket(np.maximum(dist, 0) * d)
            for h in range(8):
                vals = rel_bias[bucket, g * 8 + h]
                out[:, g, h, j, :] = np.where(valid, vals, np.float32(-30000.0))
    return np.ascontiguousarray(out.reshape(128, 3 * 8 * 256))


def make_in_maps(inp):
    f = lambda a: np.ascontiguousarray(np.asarray(a, dtype=np.float32))
    pk = lambda v: f(np.asarray(v).reshape(-1, 128).T)
    rep = lambda v: f(np.tile(np.asarray(v).reshape(1, -1), (128, 1)))
    s_ = np.arange(128)
    tri = (s_[:, None] <= s_[None, :]).astype(np.float32)
    shared = {
        "w_in": f(inp["w_in"][0]),
        "g_mix": pk(inp["norm_mix_g"][0]),
        "b_if": rep(inp["b_gate_if"][0]),
        "conv_w": f(np.asarray(inp["conv_w"][0]).T.reshape(16, 128, 4).transpose(1, 0, 2)),
        "conv_b": pk(inp["conv_b"][0]),
        "g_ml": pk(inp["mlstm_norm_g"][0]),
        "w_a": f(inp["w_proj_a"][0]), "w_b": f(inp["w_proj_b"][0]), "w_o": f(inp["w_out"][0]),
        "g_ffn": pk(inp["norm_ffn_g"][0]),
        "w_gate": f(inp["w_gate"][0]), "w_up": f(inp["w_up"][0]), "w_down": f(inp["w_down"][0]),
        "g_fin": rep(inp["norm_final_g"]),
        "biasT": _bias_tables(np.asarray(inp["rel_bias"], dtype=np.float32)),
        "c_ident": np.eye(128, dtype=np.float32).astype(ml_dtypes.bfloat16),
        "c_identf": np.eye(128, dtype=np.float32),
        "c_tri": tri,
        "c_negm": np.where(tri > 0, 0.0, -30000.0).astype(np.float32),
        "c_ones": np.ones((128, 128), np.float32),
    }
    x = np.asarray(inp["x"], dtype=np.float32)
    return [dict(shared, x=np.ascontiguousarray(x[b])) for b in range(x.shape[0])]


_PROG = None


def kernel(**inputs):
    global _PROG
    if _PROG is None:
        _PROG = Prog()
    in_maps = make_in_maps(inputs)
    res = run_bass_kernel_spmd(_PROG.nc, in_maps, core_ids=list(range(8)))
    return np.stack([np.asarray(r["y"], dtype=np.float32) for r in res.results], axis=0)
```
